# Optimizing a Trainium2 kernel written in Bass

```python
import jax, jax.numpy as jnp
from jax import lax
import numpy as np

D_MODEL = 2048
BATCH = 4
SEQ = 2048
DEPTH = 2
DEC_BATCH = 128
DEC_SEQ = 4
PAST_LEN = 16384
PAGE_SIZE = 128

N_BRANCH = 4
D_BRANCH = D_MODEL // 4
D_CONF = D_BRANCH
CONF_K = 31
D_GMLP = D_BRANCH
GMLP_HEADS = 4
GMLP_HEAD_DIM = D_GMLP // GMLP_HEADS
CHUNK = 128
D_SCONV = D_BRANCH
SCONV_K = 3
D_POOL = D_BRANCH
POOL_WINDOWS = (2, 4, 8, 16)
POOL_GROUPS = 4
POOL_GROUP_DIM = D_POOL // POOL_GROUPS
POOL_PAST = 15
D_FF = 11 * D_MODEL // 4
FFN_K = 3
EPS = 1e-6
IN_SIZES = (D_CONF, D_CONF, D_GMLP, D_GMLP, D_SCONV, D_SCONV, D_SCONV, D_POOL, N_BRANCH * D_MODEL)
IN_WIDTH = 2 * D_CONF + 2 * D_GMLP + 3 * D_SCONV + D_POOL + N_BRANCH * D_MODEL

kernel_name = 'hybrid_conv_gmlp_pool_decoder_step'


def rmsnorm(x, g):
    xf = x.astype(jnp.float32)
    y = xf * lax.rsqrt(jnp.mean(xf * xf, axis=-1, keepdims=True) + EPS)
    return y.astype(x.dtype) * g.astype(x.dtype)


def layernorm(x, g, b):
    xf = x.astype(jnp.float32)
    mu = jnp.mean(xf, axis=-1, keepdims=True)
    var = jnp.mean(jnp.square(xf - mu), axis=-1, keepdims=True)
    y = (xf - mu) * lax.rsqrt(var + EPS)
    return y.astype(x.dtype) * g.astype(x.dtype) + b.astype(x.dtype)


def causal_dwconv(x, past, w):
    K, C = w.shape
    xp = jnp.concatenate([past.astype(x.dtype), x], axis=1)
    y = lax.conv_general_dilated(xp, w[:, None, :].astype(x.dtype), window_strides=(1,),
                                 padding='VALID', dimension_numbers=('NWC', 'WIO', 'NWC'),
                                 feature_group_count=C)
    return y, xp[:, -(K - 1):]


def chunk_spatial_gate(u, v, ws, bias):
    B, T, _ = v.shape
    L = min(T, CHUNK)
    n = T // L
    vh = v.reshape(B, n, L, GMLP_HEADS, GMLP_HEAD_DIM)
    w = jnp.tril(ws[:, :L, :L]).astype(v.dtype)
    mixed = jnp.einsum('hts,bcshd->bcthd', w, vh) + bias[:, :L].T[:, :, None].astype(v.dtype)
    return u * mixed.reshape(B, T, D_GMLP)


def multiscale_pool(p, past, start_pos):
    T = p.shape[1]
    xp = jnp.concatenate([past.astype(p.dtype), p], axis=1)
    xf = xp.astype(jnp.float32)
    cs = jnp.concatenate([jnp.zeros_like(xf[:, :1]), jnp.cumsum(xf, axis=1)], axis=1)
    pos = start_pos + jnp.arange(T, dtype=jnp.int32)
    outs = []
    for g, win in enumerate(POOL_WINDOWS):
        c0, c1 = g * POOL_GROUP_DIM, (g + 1) * POOL_GROUP_DIM
        hi = cs[:, POOL_PAST + 1:POOL_PAST + 1 + T, c0:c1]
        lo = cs[:, POOL_PAST + 1 - win:POOL_PAST + 1 - win + T, c0:c1]
        cnt = jnp.minimum(win, pos + 1).astype(jnp.float32)
        outs.append((hi - lo) / cnt[None, :, None])
    y = jnp.concatenate(outs, axis=-1) - xf[:, POOL_PAST:]
    return y.astype(p.dtype), xp[:, -POOL_PAST:]


def trunk(x, start_pos, past_conf, past_sconv, past_pool, past_ffn,
          norm_mix, w_in, conf_dw, conf_ln_g, conf_ln_b, gmlp_ln_g, gmlp_ln_b, gmlp_ws, gmlp_b,
          sconv_dw, pool_w, pool_scale, w_branch, w_o, norm_ffn, ffn_up, ffn_dw, ffn_down, norm_final):
    B, T, _ = x.shape
    split_points = np.cumsum(IN_SIZES)[:-1].tolist()
    s_conf, s_sconv, s_pool, s_ffn, s_v = [], [], [], [], []
    for l in range(DEPTH):
        xn = rmsnorm(x, norm_mix[l])
        a_in, a_gate, g_u, g_v, c_b, c_c, c_h, p_in, gate_logits = jnp.split(xn @ w_in[l], split_points, axis=-1)
        a, st = causal_dwconv(a_in * jax.nn.sigmoid(a_gate), past_conf[l], conf_dw[l])
        s_conf.append(st)
        br_a = jax.nn.silu(layernorm(a, conf_ln_g[l], conf_ln_b[l])) @ w_branch[l, 0]
        u = jax.nn.gelu(g_u)
        v = layernorm(jax.nn.gelu(g_v), gmlp_ln_g[l], gmlp_ln_b[l])
        s_v.append(v)
        br_b = chunk_spatial_gate(u, v, gmlp_ws[l], gmlp_b[l]) @ w_branch[l, 1]
        z, st = causal_dwconv(c_c * c_h, past_sconv[l], sconv_dw[l])
        s_sconv.append(st)
        br_c = (c_b * z) @ w_branch[l, 2]
        pm, st = multiscale_pool(p_in, past_pool[l], start_pos)
        s_pool.append(st)
        pm = jnp.einsum('btgc,gce->btge', pm.reshape(B, T, POOL_GROUPS, POOL_GROUP_DIM),
                        pool_w[l]).reshape(B, T, D_POOL) * pool_scale[l]
        br_d = pm @ w_branch[l, 3]
        g = jax.nn.sigmoid(gate_logits).reshape(B, T, N_BRANCH, D_MODEL)
        merged = g[:, :, 0] * br_a + g[:, :, 1] * br_b + g[:, :, 2] * br_c + g[:, :, 3] * br_d
        x = x + merged @ w_o[l]
        h, st = causal_dwconv(rmsnorm(x, norm_ffn[l]) @ ffn_up[l], past_ffn[l], ffn_dw[l])
        s_ffn.append(st)
        h_g, h_v = jnp.split(h, 2, axis=-1)
        x = x + (jax.nn.silu(h_g) * h_v) @ ffn_down[l]
    y = rmsnorm(x, norm_final)
    return y, jnp.stack(s_conf), jnp.stack(s_sconv), jnp.stack(s_pool), jnp.stack(s_ffn), jnp.stack(s_v)


def setup_inputs(seed: int = 0) -> dict:
    key = jax.random.key(seed)
    ks = jax.random.split(key, 32)
    nrm = lambda k, shape, s: jax.random.normal(k, shape, jnp.float32) * s
    ones_n = lambda k, shape: 1.0 + 0.1 * jax.random.normal(k, shape, jnp.float32)
    return {
        'x_prompt': nrm(ks[0], (BATCH, SEQ, D_MODEL), 1.0),
        'x_sample': nrm(ks[1], (DEC_BATCH, DEC_SEQ, D_MODEL), 1.0),
        'state_conf_conv': nrm(ks[2], (DEPTH, DEC_BATCH, CONF_K - 1, D_CONF), 0.5),
        'state_sconv': nrm(ks[3], (DEPTH, DEC_BATCH, SCONV_K - 1, D_SCONV), 0.5),
        'state_pool': nrm(ks[4], (DEPTH, DEC_BATCH, POOL_PAST, D_POOL), 0.5),
        'state_ffn_conv': nrm(ks[5], (DEPTH, DEC_BATCH, FFN_K - 1, 2 * D_FF), 0.5),
        'norm_mix': ones_n(ks[6], (DEPTH, D_MODEL)),
        'w_in': nrm(ks[7], (DEPTH, D_MODEL, IN_WIDTH), D_MODEL ** -0.5),
        'conf_dw': nrm(ks[8], (DEPTH, CONF_K, D_CONF), CONF_K ** -0.5),
        'conf_ln_g': ones_n(ks[9], (DEPTH, D_CONF)),
        'conf_ln_b': nrm(ks[10], (DEPTH, D_CONF), 0.02),
        'gmlp_ln_g': ones_n(ks[11], (DEPTH, D_GMLP)),
        'gmlp_ln_b': nrm(ks[12], (DEPTH, D_GMLP), 0.02),
        'gmlp_ws': nrm(ks[13], (DEPTH, GMLP_HEADS, CHUNK, CHUNK), CHUNK ** -0.5),
        'gmlp_b': ones_n(ks[14], (DEPTH, GMLP_HEADS, CHUNK)),
        'sconv_dw': nrm(ks[15], (DEPTH, SCONV_K, D_SCONV), SCONV_K ** -0.5),
        'pool_w': nrm(ks[16], (DEPTH, POOL_GROUPS, POOL_GROUP_DIM, POOL_GROUP_DIM), POOL_GROUP_DIM ** -0.5),
        'pool_scale': ones_n(ks[17], (DEPTH, D_POOL)),
        'w_branch': nrm(ks[18], (DEPTH, N_BRANCH, D_BRANCH, D_MODEL), D_BRANCH ** -0.5),
        'w_o': nrm(ks[19], (DEPTH, D_MODEL, D_MODEL), D_MODEL ** -0.5),
        'norm_ffn': ones_n(ks[20], (DEPTH, D_MODEL)),
        'ffn_up': nrm(ks[21], (DEPTH, D_MODEL, 2 * D_FF), D_MODEL ** -0.5),
        'ffn_dw': nrm(ks[22], (DEPTH, FFN_K, 2 * D_FF), FFN_K ** -0.5),
        'ffn_down': nrm(ks[23], (DEPTH, D_FF, D_MODEL), D_FF ** -0.5),
        'norm_final': ones_n(ks[24], (D_MODEL,)),
    }


def reference(x_prompt, x_sample, state_conf_conv, state_sconv, state_pool, state_ffn_conv,
              norm_mix, w_in, conf_dw, conf_ln_g, conf_ln_b, gmlp_ln_g, gmlp_ln_b, gmlp_ws, gmlp_b,
              sconv_dw, pool_w, pool_scale, w_branch, w_o, norm_ffn, ffn_up, ffn_dw, ffn_down, norm_final):
    weights = (norm_mix, w_in, conf_dw, conf_ln_g, conf_ln_b, gmlp_ln_g, gmlp_ln_b, gmlp_ws, gmlp_b,
               sconv_dw, pool_w, pool_scale, w_branch, w_o, norm_ffn, ffn_up, ffn_dw, ffn_down, norm_final)
    dt = x_prompt.dtype
    zc = jnp.zeros((DEPTH, BATCH, CONF_K - 1, D_CONF), dt)
    zs = jnp.zeros((DEPTH, BATCH, SCONV_K - 1, D_SCONV), dt)
    zp = jnp.zeros((DEPTH, BATCH, POOL_PAST, D_POOL), dt)
    zf = jnp.zeros((DEPTH, BATCH, FFN_K - 1, 2 * D_FF), dt)
    y_prompt, conf_p, sconv_p, pool_p, ffn_p, _ = trunk(x_prompt, 0, zc, zs, zp, zf, *weights)
    y_sample, conf_s, sconv_s, pool_s, ffn_s, v_s = trunk(
        x_sample, PAST_LEN, state_conf_conv, state_sconv, state_pool, state_ffn_conv, *weights)
    return (y_prompt, y_sample, conf_p, conf_s, sconv_p, sconv_s, pool_p, pool_s, ffn_p, ffn_s, v_s)
```

```python
import numpy as np
import concourse.bass as bass
import concourse.mybir as mybir
from concourse.bass_utils import run_bass_kernel_spmd

F32 = mybir.dt.float32
BF16 = mybir.dt.bfloat16
AF = mybir.ActivationFunctionType
ALU = mybir.AluOpType
EPS = 1e-6
NCORE = 8
TPW = 1152
NSQ = 16
PASSES = [(640, 0, 0), (512, 16, 640)]
ENGS = ("pe", "act", "dve", "pool", "sp")
A_IN, A_GATE, G_U, G_V, C_B, C_C, C_H, P_IN, GATES = 0, 512, 1024, 1536, 2048, 2560, 3072, 3584, 4096
O_NM, O_NF, O_NFIN, O_CDW, O_CLG, O_CLB, O_GLG, O_GLB, O_SDW, O_PSC, O_FDW, NPRM = (
    0, 32, 64, 80, 328, 336, 344, 352, 360, 384, 392, 920)


class _Op:
    __slots__ = ("eng", "fn", "deps", "dma", "ndma", "sig", "val", "pos")


class Tracker:
    def __init__(self):
        self.streams = {e: [] for e in ENGS}
        self.wr = {}
        self.acc = {}
        self.chan_cnt = {}
        self.chan_last = {}
        self.npos = 0

    def op(self, eng, fn, r=(), w=(), dma=None, ndma=1):
        o = _Op()
        o.eng, o.fn, o.dma, o.ndma, o.sig, o.val = eng, fn, dma, ndma, False, 0
        deps = []
        for k in r:
            deps += self.wr.get(k, [])
        for k in w:
            deps += self.wr.get(k, [])
            deps += list(self.acc.get(k, {}).values())
        sid = dma if dma else eng
        for k in r:
            self.acc.setdefault(k, {})[sid] = o
        for k in w:
            self.acc[k] = {sid: o}
            self.wr[k] = [o]
        if dma:
            if dma in self.chan_last:
                deps.append(self.chan_last[dma])
            self.chan_last[dma] = o
            self.chan_cnt[dma] = self.chan_cnt.get(dma, 0) + ndma
            o.val = 16 * self.chan_cnt[dma]
        o.deps = deps
        self.npos += 1
        o.pos = self.npos
        self.streams[eng].append(o)
        return o

    def finalize(self):
        for e in ENGS:
            for o in self.streams[e]:
                for d in o.deps:
                    if d.dma is None and not (d.eng == "pe" and o.eng == "pe" and o.dma is None):
                        d.sig = True
        for e in ENGS:
            c = 0
            for o in self.streams[e]:
                if o.dma is None and o.sig:
                    c += 1
                    o.val = c

    def emit(self, eng_name, e, sems):
        waited = {}
        for o in self.streams[eng_name]:
            need = {}
            for d in o.deps:
                if d.dma is None and d.eng == "pe" and eng_name == "pe" and o.dma is None:
                    continue
                sn = d.dma if d.dma else "E" + d.eng
                if d.val > need.get(sn, 0):
                    need[sn] = d.val
            for sn, v in need.items():
                if waited.get(sn, 0) < v:
                    e.wait_ge(sems[sn], v)
                    waited[sn] = v
            ins = o.fn(e)
            if o.dma:
                for i in ins:
                    i.then_inc(sems[o.dma], 16)
            elif o.sig:
                ins.then_inc(sems["E" + eng_name], 1)


def build_nc():
    nc = bass.Bass("TRN2", target_bir_lowering=False)
    tr = Tracker()

    def din(name, shape):
        return nc.dram_tensor(name, shape, F32, kind="ExternalInput").ap()

    def dout(name, shape):
        return nc.dram_tensor(name, shape, F32, kind="ExternalOutput").ap()

    xw = din("xw", [1216, 2048])
    sc_in = din("sc", [2, 480, 512])
    ss_in = din("ss", [2, 32, 512])
    sp_in = din("sp", [2, 240, 512])
    sf_in = din("sf", [2, 32, 11264])
    w_in = din("w_in", [2, 2048, 12288])
    w_br = din("w_br", [2, 2048, 2048])
    w_o = din("w_o", [2, 2048, 2048])
    f_up = din("f_up", [2, 2048, 11264])
    f_dn = din("f_dn", [2, 5632, 2048])
    prm_in = din("prm", [128, NPRM])
    cst_in = din("cst", [128, 320])
    wst_in = din("wst", [128, 1024])
    gbb_in = din("gbb", [128, 1024])
    sw_in = din("sw", [128, 160])
    pw_in = din("pw", [128, 1024])
    y_out = dout("y", [1216, 2048])
    oc_out = dout("oc", [2, 510, 512])
    os_out = dout("os", [2, 34, 512])
    op_out = dout("op", [2, 255, 512])
    of_out = dout("of", [2, 34, 11264])
    ov_out = dout("ov", [2, 64, 512])

    NW = 53200
    AR = nc.alloc_sbuf_tensor("AR", [128, NW], F32)
    PSt = nc.alloc_psum_tensor("PS", [128, 8, 512], F32)
    PSG = [PSt[:, 2 * g:2 * g + 2, :].rearrange("p a b -> p (a b)") for g in range(4)]

    class Mem:
        def __init__(self, base):
            self.p = base

        def f(self, n):
            o = self.p
            self.p += n
            assert self.p <= NW, ("SBUF overflow", self.p)
            return o

    def fv(off, n):
        return AR[:, off:off + n]

    def bv(off, n):
        return AR[:, off:off + (n + 1) // 2].bitcast(BF16)

    pm = Mem(0)
    X = fv(pm.f(16 * 640), 16 * 640).rearrange("p (c t) -> p c t", c=16)
    XN = bv(pm.f(8 * 640), 16 * 640).rearrange("p (c t) -> p c t", c=16)
    WR = [bv(pm.f(4096), 8192) for _ in range(3)]
    PRM = fv(pm.f(NPRM), NPRM)
    CST = fv(pm.f(320), 320)
    IDENT = CST[:, 0:128]
    MASK = CST[:, 128:256]
    INV = CST[:, 256:320].rearrange("p (g j) -> p g j", g=4)
    _oh = pm.f(128)
    OH = AR[:, _oh:_oh + 128].bitcast(BF16)
    ONESA_H = OH[:, 0:128]
    ONESB_H = OH[:, 128:256]
    ONESB = fv(pm.f(128), 128)
    WT = bv(pm.f(512), 1024).rearrange("p (i t) -> p i t", i=8)
    POOLW = bv(pm.f(512), 1024).rearrange("p (i t) -> p i t", i=8)
    GB = fv(pm.f(1024), 1024).rearrange("p (i t) -> p i t", i=8)
    SW = fv(pm.f(160), 160)
    HISTA = fv(pm.f(240), 240).rearrange("p (l c k) -> p l c k", l=2, c=4)
    HISTC = fv(pm.f(16), 16).rearrange("p (l c k) -> p l c k", l=2, c=4)
    HISTD = fv(pm.f(120), 120).rearrange("p (l c k) -> p l c k", l=2, c=4)
    HISTF = fv(pm.f(352), 352).rearrange("p (l c k) -> p l c k", l=2, c=88)
    MEAN = fv(pm.f(640), 640)
    RSTD = fv(pm.f(640), 640)
    EPSB = CST[:, 128:136]
    IOB = [fv(pm.f(512), 512) for _ in range(4)]
    ARENA = pm.p

    REG = {}

    def areg(key, off, n):
        REG[key] = (off, off + n)
        return off

    am = Mem(ARENA)
    _o = am.f(8 * 640)
    MIX = bv(_o, 16 * 640).rearrange("p (c t) -> p c t", c=16)
    for i_ in range(16):
        areg("MIX%d" % i_, _o + 320 * i_, 320)
    after_mix = am.p
    _o = am.f(4 * 640)
    S2 = fv(_o, 4 * 640).rearrange("p (c t) -> p c t", c=4)
    for i_ in range(4):
        areg("S2_%d" % i_, _o + 640 * i_, 640)
    GLUB = [fv(areg("GLUB%d" % i_, am.f(1086), 1086), 1086) for i_ in range(2)]
    PBUF = fv(areg("PBUF", am.f(831), 831), 831)
    SCBUF = PBUF[:, 0:642]
    PMB = [bv(areg("PMB%d" % i_, am.f(320), 320), 640) for i_ in range(2)]
    TMPS = [fv(areg("TMP%d" % i_, am.f(832), 832), 832) for i_ in range(6)]
    _vtb = am.f(640)
    areg("VTB0", _vtb, 320)
    areg("VTB1", _vtb + 320, 320)
    VTB = [bv(_vtb, 640), bv(_vtb + 320, 640)]
    STGM = fv(areg("STGM", am.f(510), 510), 510)
    m1_end = am.p
    LNA_SQ = [bv(REG["GLUB0"][0], 640), bv(REG["GLUB0"][0] + 320, 640)]
    LNA_T = fv(REG["GLUB1"][0], 640)
    am = Mem(after_mix)
    _o = am.f(8 * 640)
    MG = bv(_o, 16 * 640).rearrange("p (c t) -> p c t", c=16)
    for i_ in range(16):
        areg("MG%d" % i_, _o + 320 * i_, 320)
    SIG = [fv(areg("SIG%d" % i_, am.f(640), 640), 640) for i_ in range(2)]
    PROD = fv(areg("PROD", am.f(640), 640), 640)
    ACC = [fv(areg("ACC%d" % i_, am.f(640), 640), 640) for i_ in range(4)]
    BRW = [bv(areg("BRW%d" % i_, am.f(1024), 1024), 2048).rearrange("p (k n) -> p k n", k=4) for i_ in range(3)]
    m2_end = am.p
    am = Mem(ARENA)
    HSF = fv(areg("HSF", am.f(88 * 32), 88 * 32), 88 * 32).rearrange("p (c s k) -> p c s k", c=88, s=16)
    _o = am.f(4 * 640)
    SGB = fv(_o, 4 * 640).rearrange("p (c t) -> p c t", c=4)
    for i_ in range(4):
        areg("SGB%d" % i_, _o + 640 * i_, 640)
    ACTB = bv(areg("ACTB", am.f(2 * 640), 2 * 640), 4 * 640).rearrange("p (c t) -> p c t", c=4)
    HB = [fv(areg("HB%d" % i_, am.f(642), 642), 642) for i_ in range(2)]
    CVB = [fv(areg("CVB%d" % i_, am.f(640), 640), 640) for i_ in range(2)]
    FT = [fv(areg("FT%d" % i_, am.f(640), 640), 640) for i_ in range(2)]
    STG = [fv(areg("STG%d" % i_, am.f(4 * 34), 4 * 34), 4 * 34).rearrange("p (c k) -> p c k", c=4) for i_ in range(2)]
    _o = am.f(4 * 640)
    YN = fv(_o, 4 * 640).rearrange("p (c t) -> p c t", c=4)
    for i_ in range(4):
        areg("YN%d" % i_, _o + 640 * i_, 640)
    YOB = [fv(areg("YOB%d" % i_, am.f(512), 512), 512) for i_ in range(4)]
    ffn_end = am.p
    ARENA_KEYS_M1 = (["S2_%d" % c for c in range(4)] + ["GLUB0", "GLUB1", "PBUF", "PMB0", "PMB1",
                     "VTB0", "VTB1", "STGM"] + ["TMP%d" % i for i in range(6)])
    ARENA_KEYS_M2 = (["MG%d" % c for c in range(16)] + ["SIG0", "SIG1", "PROD"] + ["ACC%d" % i for i in range(4)]
                     + ["BRW%d" % i for i in range(3)])
    ARENA_KEYS_FFN = (["HSF", "SGB0", "SGB1", "SGB2", "SGB3", "ACTB", "HB0", "HB1", "CVB0", "CVB1", "FT0", "FT1",
                       "STG0", "STG1"] + ["YN%d" % c for c in range(4)] + ["YOB%d" % i for i in range(4)])
    MIXK = ["MIX%d" % i for i in range(16)]
    ALL_ARENA_KEYS = ARENA_KEYS_M1 + ARENA_KEYS_M2 + ARENA_KEYS_FFN + MIXK

    def phase_switch(new_keys, old_keys=None):
        flush()
        upd = {}
        for nk in new_keys:
            lo, hi = REG[nk]
            best = {}
            for k in ALL_ARENA_KEYS:
                klo, khi = REG[k]
                if klo < hi and lo < khi:
                    for o in tr.wr.get(k, []) + list(tr.acc.get(k, {}).values()):
                        sid = o.dma if o.dma else o.eng
                        if sid not in best or o.pos > best[sid].pos:
                            best[sid] = o
            upd[nk] = best
        for nk, best in upd.items():
            tr.wr[nk] = list(best.values())
            tr.acc[nk] = dict(best)

    def ACT(out, in_, func, r, w, **kw):
        tr.op("act", lambda e: e.activation(out=out, in_=in_, func=func, **kw), r, w)

    def TT(out, in0, in1, op, r, w, eng="dve"):
        tr.op(eng, lambda e: e.tensor_tensor(out=out, in0=in0, in1=in1, op=op), r, w)

    def TS(out, in0, s1, s2, op0, op1, r, w, eng="dve"):
        if s2 is None:
            tr.op(eng, lambda e: e.tensor_scalar(out=out, in0=in0, scalar1=s1, scalar2=None, op0=op0), r, w)
        else:
            tr.op(eng, lambda e: e.tensor_scalar(out=out, in0=in0, scalar1=s1, scalar2=s2, op0=op0, op1=op1), r, w)

    def STT(out, in0, scalar, in1, op0, op1, r, w, eng="dve"):
        tr.op(eng, lambda e: e.scalar_tensor_tensor(out=out, in0=in0, scalar=scalar, in1=in1, op0=op0, op1=op1),
              r, w)

    def CP(out, in_, r, w, eng="dve"):
        if eng == "act":
            ACT(out, in_, AF.Copy, r, w)
        else:
            tr.op(eng, lambda e: e.tensor_copy(out=out, in_=in_), r, w)

    def MSET(ap, val, w, eng="dve"):
        tr.op(eng, lambda e: e.memset(ap, val), (), w)

    def RECIP(out, in_, r, w):
        tr.op("dve", lambda e: e.reciprocal(out=out, in_=in_), r, w)

    def MM(out, lhsT, rhs, start, stop, r, w):
        tr.op("pe", lambda e: e.matmul(out, lhsT, rhs, start=start, stop=stop), r, w)

    def TRP(out, in_, ident, r, w):
        tr.op("pe", lambda e: e.transpose(out, in_, ident), r, w)

    def DMA(eng, chan, pairs, r, w):
        tr.op(eng, lambda e: [e.dma_start(out=o, in_=i) for (o, i) in pairs], r, w, dma=chan, ndma=len(pairs))

    cnt = {"ps": 0, "wr": 0, "iob": 0, "brw": 0, "alt": 0, "yob": 0}

    def next_ps():
        g = cnt["ps"] % 4
        cnt["ps"] += 1
        return g

    iob_held = set()

    def next_iob():
        for _ in range(4):
            i = cnt["iob"] % 4
            cnt["iob"] += 1
            if i not in iob_held:
                return i
        raise RuntimeError("all IOB slots held")

    def alt_eng():
        cnt["alt"] += 1
        return "act" if cnt["alt"] % 2 else "dve"

    def wload(src_list):
        s = cnt["wr"] % 3
        cnt["wr"] += 1
        key = "WR%d" % s
        pairs = []
        k = src_list[0][1]
        ntot = sum(n for (_, _, n) in src_list)
        view = WR[s][:, 0:k * ntot].rearrange("p (k n) -> p k n", k=k)
        off = 0
        for (src, kk, n) in src_list:
            pairs.append((view[:, :, off:off + n], src.rearrange("(k p) n -> p k n", p=128)))
            off += n
        DMA("pool", "C" + key, pairs, (), [key])
        return view, key

    def tiles(T):
        return [(0, 512), (512, T)] if T > 512 else [(0, T)]

    deferred = []

    def later(fn, delay):
        deferred.append([delay, fn])

    def tick():
        for d in deferred:
            d[0] -= 1
        due = [d for d in deferred if d[0] <= 0]
        for d in due:
            deferred.remove(d)
            d[1]()

    def flush():
        while deferred:
            d = deferred.pop(0)
            d[1]()

    def proj_chunk(wview, wkey, mcol, nk, rhs_fn, rkeys, T, g):
        for k in range(nk):
            for (a, b) in tiles(T):
                MM(PSG[g][:, a:b], wview[:, k, mcol:mcol + 128], rhs_fn(k)[:, a:b], k == 0, k == nk - 1,
                   [wkey] + rkeys(k), ["PS%d" % g])

    def stats_sum(src_aps, skeys, ones, T, g, square, tmpi):
        n = len(src_aps)
        for c in range(n):
            if square:
                ti = tmpi[c % 2]
                sqb = TMPS[ti].bitcast(BF16)
                ACT(sqb[:, 0:T], src_aps[c], AF.Square, [skeys[c]], ["TMP%d" % ti])
                rhs, rk = sqb, "TMP%d" % ti
            elif tmpi is not None:
                ti = tmpi[c % 2]
                cpb = TMPS[ti].bitcast(BF16)
                ACT(cpb[:, 0:T], src_aps[c], AF.Copy, [skeys[c]], ["TMP%d" % ti])
                rhs, rk = cpb, "TMP%d" % ti
            else:
                rhs, rk = src_aps[c], skeys[c]
            for (a, b) in tiles(T):
                MM(PSG[g][:, a:b], ones, rhs[:, a:b], c == 0, c == n - 1, [rk], ["PS%d" % g])

    def rmsnorm(T, gcol, out_fn, okey_fn, tmpk):
        g = next_ps()
        srcs = [X[:, c, 0:T] for c in range(16)]
        keys = ["X%d" % c for c in range(16)]
        if tmpk == "ffn":
            for c in range(16):
                ti = c % 2
                fb_ = FT[ti].bitcast(BF16)
                ACT(fb_[:, 0:T], srcs[c], AF.Square, [keys[c]], ["FT%d" % ti])
                for (a, b) in tiles(T):
                    MM(PSG[g][:, a:b], ONESA_H, fb_[:, a:b], c == 0, c == 15, ["FT%d" % ti], ["PS%d" % g])
        else:
            stats_sum(srcs, keys, ONESA_H, T, g, True, (0, 1))
        ACT(RSTD[:, 0:T], PSG[g][:, 0:T], AF.Sqrt, [], ["PS%d" % g, "RSTD"], bias=EPSB[:, 0:1], scale=1.0)
        RECIP(RSTD[:, 0:T], RSTD[:, 0:T], [], ["RSTD"])
        for c in range(16):
            STT(out_fn(c), X[:, c, 0:T], PRM[:, gcol + c:gcol + c + 1], RSTD[:, 0:T], ALU.mult, ALU.mult,
                ["X%d" % c, "RSTD"], [okey_fn(c)])

    def layernorm4(T, gcol, bcol, func, out_fn, okey_fn):
        g1, g2 = next_ps(), next_ps()
        srcs = [S2[:, c, 0:T] for c in range(4)]
        keys = ["S2_%d" % c for c in range(4)]
        stats_sum(srcs, keys, ONESB_H, T, g1, False, (0, 1))
        stats_sum(srcs, keys, ONESB_H, T, g2, True, (0, 1))
        ACT(MEAN[:, 0:T], PSG[g1][:, 0:T], AF.Copy, [], ["PS%d" % g1, "MEAN"])
        TT(TMPS[2][:, 0:T], MEAN[:, 0:T], MEAN[:, 0:T], ALU.mult, ["MEAN"], ["TMP2"])
        TT(TMPS[3][:, 0:T], PSG[g2][:, 0:T], TMPS[2][:, 0:T], ALU.subtract, ["TMP2"], ["PS%d" % g2, "TMP3"])
        ACT(RSTD[:, 0:T], TMPS[3][:, 0:T], AF.Sqrt, ["TMP3"], ["RSTD"], bias=EPSB[:, 0:1], scale=1.0)
        RECIP(RSTD[:, 0:T], RSTD[:, 0:T], [], ["RSTD"])
        for c in range(4):
            ti = c % 2
            TT(TMPS[ti][:, 0:T], S2[:, c, 0:T], MEAN[:, 0:T], ALU.subtract, ["S2_%d" % c, "MEAN"], ["TMP%d" % ti])
            TT(TMPS[ti][:, 0:T], TMPS[ti][:, 0:T], RSTD[:, 0:T], ALU.mult, ["RSTD"], ["TMP%d" % ti])
            ACT(out_fn(c), TMPS[ti][:, 0:T], func, ["TMP%d" % ti], [okey_fn(c)],
                scale=PRM[:, gcol + c:gcol + c + 1], bias=PRM[:, bcol + c:bcol + c + 1])

    def conv(out, okey, buf, bkey, K, hist, TP, NS, wcol):
        L = hist + 4
        for k in range(K):
            wk = PRM[:, wcol + k:wcol + k + 1]
            srcs = [(out[:, 0:TP], buf[:, k:k + TP])]
            if NS:
                sb = buf[:, hist + TP:hist + TP + NS * L].rearrange("p (s l) -> p s l", l=L)
                srcs.append((out[:, TP:TP + 4 * NS].rearrange("p (s l) -> p s l", l=4), sb[:, :, k:k + 4]))
            for (o, i) in srcs:
                if k == 0:
                    TS(o, i, wk, None, ALU.mult, None, [bkey], [okey])
                else:
                    STT(o, i, wk, o, ALU.mult, ALU.add, [bkey], [okey])

    def split_evac(buf, bkey, hist, TP, NS, emit):
        emit(buf[:, hist:hist + TP], 0, TP, False)
        if NS:
            L = hist + 4
            sb = buf[:, hist + TP:hist + TP + NS * L].rearrange("p (s l) -> p s l", l=L)
            emit(sb[:, :, hist:hist + 4], TP, TP + 4 * NS, True)

    def v3(ap, is_sample):
        return ap.rearrange("p (s l) -> p s l", l=4) if is_sample else ap

    def rows_out_multi(items, dsts):
        g = next_ps()
        nmax = max(n for (_, _, n) in items)
        for j, (ap, key, n) in enumerate(items):
            TRP(PSG[g][0:n, j * 128:(j + 1) * 128], ap, IDENT, [key], ["PS%d" % g])
        i = next_iob()
        CP(IOB[i][0:nmax, 0:128 * len(items)], PSG[g][0:nmax, 0:128 * len(items)], [],
           ["PS%d" % g, "IOB%d" % i], eng=alt_eng())
        DMA("sp", "CIOB%d" % i, [(dsts[j], IOB[i][0:items[j][2], j * 128:(j + 1) * 128])
                                 for j in range(len(items))], ["IOB%d" % i], [])

    def rows_in_load(srcs, n):
        i = next_iob()
        iob_held.add(i)
        DMA("sp", "CIOB%d" % i, [(IOB[i][0:n, j * 128:(j + 1) * 128], srcs[j]) for j in range(len(srcs))], [],
            ["IOB%d" % i])
        return i

    def rows_in_multi(srcs, n, dsts, k, i=None):
        if i is None:
            i = rows_in_load(srcs, n)
        iob_held.discard(i)
        g = next_ps()
        for j in range(len(srcs)):
            TRP(PSG[g][:, j * 128:j * 128 + n], IOB[i][0:n, j * 128:(j + 1) * 128], IDENT[0:n, 0:n],
                ["IOB%d" % i], ["PS%d" % g])
        for j in range(len(srcs)):
            dst, dkey = dsts[j]
            CP(dst, PSG[g][:, j * 128:j * 128 + n].rearrange("p (s k) -> p s k", k=k), [],
               ["PS%d" % g, dkey], eng=alt_eng())

    def state_out(stg, skey, nrows, dst, ccol):
        r = 0
        items, dsts = [], []
        while r < nrows:
            n = min(120, nrows - r)
            items.append((stg[:, r:r + n], skey, n))
            dsts.append(dst[r:r + n, ccol:ccol + 128])
            r += n
            if len(items) == 4 or r >= nrows:
                rows_out_multi(items, dsts)
                items, dsts = [], []

    DMA("sp", "CSET", [(PRM, prm_in), (CST, cst_in), (GB.rearrange("p i t -> p (i t)"), gbb_in), (SW, sw_in)],
        [], ["PRM", "CST", "GB", "SW"])
    MSET(ONESA_H, 1.0 / 2048.0, ["ONESA"])
    MSET(ONESB_H, 1.0 / 512.0, ["ONESA"])
    MSET(ONESB, 1.0 / 512.0, ["ONESB"])
    MSET(HISTA.rearrange("p l c k -> p (l c k)"), 0.0, ["HA0", "HA1"])
    MSET(HISTC.rearrange("p l c k -> p (l c k)"), 0.0, ["HC0", "HC1"])
    MSET(HISTD.rearrange("p l c k -> p (l c k)"), 0.0, ["HD0", "HD1"])
    MSET(HISTF.rearrange("p l c k -> p (l c k)"), 0.0, ["HF0", "HF1"])
    DMA("sp", "CSET2", [(TMPS[0][:, 0:512], wst_in[:, 0:512]), (TMPS[1][:, 0:512], wst_in[:, 512:1024]),
                        (TMPS[2][:, 0:512], pw_in[:, 0:512]), (TMPS[3][:, 0:512], pw_in[:, 512:1024])],
        [], ["TMP0", "TMP1", "TMP2", "TMP3"])
    for i in range(8):
        TT(WT[:, i, :], TMPS[i // 4][:, (i % 4) * 128:(i % 4 + 1) * 128], MASK, ALU.mult,
           ["TMP%d" % (i // 4), "CST"], ["WT"])
    for i in range(2):
        CP(POOLW[:, 4 * i:4 * i + 4, :].rearrange("p i t -> p (i t)"), TMPS[2 + i][:, 0:512],
           ["TMP%d" % (2 + i)], ["POOLW"], eng="act")
    MSET(EPSB, EPS, ["EPSB", "CST"])

    SETUP_KEYS = ["PRM", "CST", "GB", "SW", "ONESA", "ONESB", "EPSB", "WT", "POOLW"]
    for en in ("pe", "act", "dve"):
        tr.op(en, lambda e: e.nop(), SETUP_KEYS, ())

    def row_tiles_of(pi):
        TP_, NS_, C0_ = PASSES[pi]
        rt = [(C0_ + r * 128, 128, r * 128) for r in range(TP_ // 128)]
        if NS_:
            rt.append((TPW, 4 * NS_, TP_))
        return rt

    def xload_issue(group, use_iob):
        r0, nr, col, fb = group
        if use_iob:
            i = next_iob()
            iob_held.add(i)
            buf, key, ch, hi = IOB[i], "IOB%d" % i, "CIOB%d" % i, i
        else:
            i = cnt["yob"] % 4
            cnt["yob"] += 1
            buf, key, ch, hi = YOB[i], "YOB%d" % i, "CYOB%d" % i, None
        DMA("sp", ch, [(buf[0:nr, 0:512], xw[r0:r0 + nr, fb * 512:(fb + 1) * 512])], [], [key])
        return (buf, key, hi)

    xpre = {}
    for pi, (TP, NS, C0) in enumerate(PASSES):
        T = TP + 4 * NS
        last = (pi == len(PASSES) - 1)
        phase_switch(ARENA_KEYS_FFN, ARENA_KEYS_M1)
        row_tiles = row_tiles_of(pi)
        groups = [(r0, nr, col, fb) for (r0, nr, col) in row_tiles for fb in range(4)]
        pre = xpre.pop(pi, [])
        for gi, (r0, nr, col, fb) in enumerate(groups):
            if gi < len(pre):
                buf, key, hi = pre[gi]
            else:
                buf, key, hi = xload_issue(groups[gi], (gi % 7) >= 4)
            if hi is not None:
                iob_held.discard(hi)
            g = next_ps()
            for c in range(4):
                TRP(PSG[g][:, c * 128:c * 128 + nr], buf[0:nr, c * 128:(c + 1) * 128], IDENT[0:nr, 0:nr],
                    [key], ["PS%d" % g])
            for c in range(4):
                CP(X[:, fb * 4 + c, col:col + nr], PSG[g][:, c * 128:c * 128 + nr], [],
                   ["PS%d" % g, "X%d" % (fb * 4 + c)], eng=alt_eng())

        for l in range(2):
            phase_switch(ARENA_KEYS_M1 + MIXK, ARENA_KEYS_FFN)
            def conf_hist_load(c, l=l):
                return rows_in_load([sc_in[l][q * 120:(q + 1) * 120, c * 128:(c + 1) * 128] for q in range(4)], 120)
            rmsnorm(T, O_NM + 16 * l, lambda c: XN[:, c, 0:T], lambda c: "XN%d" % c, "m1")
            xn_rhs = lambda k: XN[:, k, :]
            xn_keys = lambda k: ["XN%d" % k]

            def wcols(col, n=128, l=l):
                return (w_in[l][:, col:col + n], 16, n)

            for cp in range(2):
                c0, c1 = 2 * cp, 2 * cp + 1
                wv, wk = wload([wcols(G_U + c0 * 128), wcols(G_V + c0 * 128), wcols(G_U + c1 * 128),
                                wcols(G_V + c1 * 128)])
                for j, c in enumerate((c0, c1)):
                    tick()
                    g = next_ps()
                    proj_chunk(wv, wk, (2 * j) * 128, 16, xn_rhs, xn_keys, T, g)
                    ACT(MIX[:, 4 + c, 0:T], PSG[g][:, 0:T], AF.Gelu_apprx_tanh, [], ["PS%d" % g, "MIX%d" % (4 + c)])
                    tick()
                    g = next_ps()
                    proj_chunk(wv, wk, (2 * j + 1) * 128, 16, xn_rhs, xn_keys, T, g)
                    ACT(S2[:, c, 0:T], PSG[g][:, 0:T], AF.Gelu_apprx_tanh, [], ["PS%d" % g, "S2_%d" % c])
            layernorm4(T, O_GLG + 4 * l, O_GLB + 4 * l, AF.Identity, lambda c: S2[:, c, 0:T],
                       lambda c: "S2_%d" % c)
            if NS:
                rows_out_multi([(S2[:, c, TP:T], "S2_%d" % c, 4 * NS) for c in range(4)],
                               [ov_out[l][:, c * 128:(c + 1) * 128] for c in range(4)])
            def gmlp_heads(l=l):
                ntc = TP // 128
                for h in range(4):
                    gA, gM = next_ps(), next_ps()
                    vt, vk = VTB[h % 2], "VTB%d" % (h % 2)
                    for tc in range(ntc):
                        TRP(PSG[gA][:, tc * 128:(tc + 1) * 128], S2[:, h, tc * 128:(tc + 1) * 128], IDENT,
                            ["S2_%d" % h], ["PS%d" % gA])
                    CP(vt[:, 0:TP], PSG[gA][:, 0:TP], [], ["PS%d" % gA, vk], eng="act")
                    for tc in range(ntc):
                        MM(PSG[gM][:, tc * 128:(tc + 1) * 128], vt[:, tc * 128:(tc + 1) * 128], WT[:, 4 * l + h, :],
                           True, True, [vk, "WT"], ["PS%d" % gM])
                    ti = 4 + h % 2
                    for tc in range(ntc):
                        TT(TMPS[ti][:, tc * 128:(tc + 1) * 128], PSG[gM][:, tc * 128:(tc + 1) * 128], GB[:, 4 * l + h, :],
                           ALU.add, ["GB"], ["PS%d" % gM, "TMP%d" % ti])
                    TT(MIX[:, 4 + h, 0:TP], TMPS[ti][:, 0:TP], MIX[:, 4 + h, 0:TP], ALU.mult, ["TMP%d" % ti],
                       ["MIX%d" % (4 + h)])
                    if NS:
                        vs = S2[:, h, TP:T].rearrange("p (s l) -> p s l", l=4)
                        mx = TMPS[3][:, 0:4 * NS].rearrange("p (s l) -> p s l", l=4)
                        wb = (4 * l + h) * 16
                        bb = 128 + (4 * l + h) * 4
                        for t in range(4):
                            TS(mx[:, :, t], vs[:, :, 0], SW[:, wb + 4 * t:wb + 4 * t + 1], SW[:, bb + t:bb + t + 1],
                               ALU.mult, ALU.add, ["S2_%d" % h, "SW"], ["TMP3"])
                            for s_ in range(1, t + 1):
                                STT(mx[:, :, t], vs[:, :, s_], SW[:, wb + 4 * t + s_:wb + 4 * t + s_ + 1], mx[:, :, t],
                                    ALU.mult, ALU.add, ["S2_%d" % h, "SW"], ["TMP3"])
                        TT(MIX[:, 4 + h, TP:T], TMPS[3][:, 0:4 * NS], MIX[:, 4 + h, TP:T], ALU.mult, ["TMP3"],
                           ["MIX%d" % (4 + h)])


            bg = []

            def pump(n):
                while n > 0 and bg:
                    try:
                        next(bg[0])
                        n -= 1
                    except StopIteration:
                        bg.pop(0)

            def drain():
                pump(1 << 30)

            def conv_gen(out, okey, buf, bkey, K, hist, wcol):
                L = hist + 4
                for k in range(K):
                    wk_ = PRM[:, wcol + k:wcol + k + 1]
                    srcs = [(out[:, 0:TP], buf[:, k:k + TP])]
                    if NS:
                        sb = buf[:, hist + TP:hist + TP + NS * L].rearrange("p (s l) -> p s l", l=L)
                        srcs.append((out[:, TP:TP + 4 * NS].rearrange("p (s l) -> p s l", l=4), sb[:, :, k:k + 4]))
                    for (o, i) in srcs:
                        if k == 0:
                            TS(o, i, wk_, None, ALU.mult, None, [bkey], [okey])
                        else:
                            STT(o, i, wk_, o, ALU.mult, ALU.add, [bkey], [okey])
                    yield

            def branch_a_pair(cp, l=l):
                c0, c1 = 2 * cp, 2 * cp + 1
                wv, wk = wload([wcols(A_IN + c0 * 128), wcols(A_GATE + c0 * 128), wcols(A_IN + c1 * 128),
                                wcols(A_GATE + c1 * 128)])
                pre_a = {}
                if NS:
                    pre_a[c0] = conf_hist_load(c0)
                    pre_a[c1] = conf_hist_load(c1)
                for j, c in enumerate((c0, c1)):
                    gb, gk = GLUB[c % 2], "GLUB%d" % (c % 2)
                    CP(gb[:, 0:30], HISTA[:, l, c, :], ["HA%d" % l], [gk])
                    if NS:
                        sbv = gb[:, 30 + TP:30 + TP + NS * 34].rearrange("p (s l) -> p s l", l=34)
                        rows_in_multi([None] * 4, 120, [(sbv[:, 4 * q:4 * q + 4, 0:30], gk) for q in range(4)], 30,
                                      i=pre_a[c])
                    tick()
                    g = next_ps()
                    proj_chunk(wv, wk, (2 * j) * 128, 16, xn_rhs, xn_keys, T, g)
                    ta, tak = TMPS[c % 2], "TMP%d" % (c % 2)
                    ACT(ta[:, 0:T], PSG[g][:, 0:T], AF.Copy, [], ["PS%d" % g, tak])
                    tick()
                    g = next_ps()
                    proj_chunk(wv, wk, (2 * j + 1) * 128, 16, xn_rhs, xn_keys, T, g)
                    tb, tbk = TMPS[2 + c % 2], "TMP%d" % (2 + c % 2)
                    ACT(tb[:, 0:T], PSG[g][:, 0:T], AF.Sigmoid, [], ["PS%d" % g, tbk])

                    def em(dst, lo, hi, smp, ta=ta, tb=tb, tak=tak, tbk=tbk, gk=gk):
                        TT(dst, v3(ta[:, lo:hi], smp), v3(tb[:, lo:hi], smp), ALU.mult, [tak, tbk], [gk])
                    split_evac(gb, gk, 30, TP, NS, em)
                    if not last:
                        CP(HISTA[:, l, c, :], gb[:, TP:TP + 30], [gk], ["HA%d" % l], eng="act")
                    else:
                        CP(STGM[:, 0:30], gb[:, TP:TP + 30], [gk], ["STGM"], eng="act")
                        CP(STGM[:, 30:30 + 30 * NS].rearrange("p (s k) -> p s k", k=30), sbv[:, :, 4:34], [gk],
                           ["STGM"], eng="act")
                        state_out(STGM, "STGM", 30 + 30 * NS, oc_out[l], c * 128)
                if cp == 0:
                    gmlp_heads()
                for c in (c0, c1):
                    bg.append(conv_gen(S2[:, c, :], "S2_%d" % c, GLUB[c % 2], "GLUB%d" % (c % 2), 31, 30,
                                       O_CDW + 124 * l + 31 * c))

            def branch_cd(c, prefetch_next, npump, l=l):
                wv, wk = wload([wcols(C_C + c * 128), wcols(C_H + c * 128), wcols(C_B + c * 128),
                                wcols(P_IN + c * 128)])
                CP(SCBUF[:, 0:2], HISTC[:, l, c, :], ["HC%d" % l], ["PBUF"])
                if NS:
                    scv = SCBUF[:, 2 + TP:2 + TP + NS * 6].rearrange("p (s l) -> p s l", l=6)
                    pv = PBUF[:, 15 + TP:15 + TP + NS * 19].rearrange("p (s l) -> p s l", l=19)
                    if "cd" not in pre_cd:
                        pre_cd["cd"] = (rows_in_load([ss_in[l][0:2 * NS, c * 128:(c + 1) * 128]], 2 * NS),
                                        rows_in_load([sp_in[l][q * 120:(q + 1) * 120, c * 128:(c + 1) * 128]
                                                      for q in range(2)], 120))
                    pre = pre_cd.pop("cd")
                    rows_in_multi([None], 2 * NS, [(scv[:, :, 0:2], "PBUF")], 2, i=pre[0])
                tick()
                g = next_ps()
                proj_chunk(wv, wk, 0, 16, xn_rhs, xn_keys, T, g)
                ACT(TMPS[0][:, 0:T], PSG[g][:, 0:T], AF.Copy, [], ["PS%d" % g, "TMP0"])
                pump(npump)
                tick()
                g = next_ps()
                proj_chunk(wv, wk, 128, 16, xn_rhs, xn_keys, T, g)

                def emc(dst, lo, hi, smp, g=g):
                    TT(dst, v3(PSG[g][:, lo:hi], smp), v3(TMPS[0][:, lo:hi], smp), ALU.mult, ["TMP0"],
                       ["PS%d" % g, "PBUF"])
                split_evac(SCBUF, "PBUF", 2, TP, NS, emc)
                conv(TMPS[1], "TMP1", SCBUF, "PBUF", 3, 2, TP, NS, O_SDW + 12 * l + 3 * c)
                pump(npump)
                tick()
                g = next_ps()
                proj_chunk(wv, wk, 256, 16, xn_rhs, xn_keys, T, g)
                TT(MIX[:, 8 + c, 0:T], PSG[g][:, 0:T], TMPS[1][:, 0:T], ALU.mult, ["TMP1"],
                   ["PS%d" % g, "MIX%d" % (8 + c)])
                if not last:
                    CP(HISTC[:, l, c, :], SCBUF[:, TP:TP + 2], ["PBUF"], ["HC%d" % l], eng="act")
                else:
                    st = TMPS[2]
                    CP(st[:, 0:2], SCBUF[:, TP:TP + 2], ["PBUF"], ["TMP2"], eng="act")
                    CP(st[:, 2:2 + 2 * NS].rearrange("p (s k) -> p s k", k=2), scv[:, :, 4:6], ["PBUF"], ["TMP2"],
                       eng="act")
                    later(lambda st=st, l=l, c=c: state_out(st, "TMP2", 2 + 2 * NS, os_out[l], c * 128), 1)
                pump(npump)
                win = 2 ** (c + 1)
                Lp = 15 + TP + NS * 19
                CP(PBUF[:, 0:15], HISTD[:, l, c, :], ["HD%d" % l], ["PBUF"], eng="act")
                if NS:
                    rows_in_multi([None, None], 120, [(pv[:, 8 * q:8 * q + 8, 0:15], "PBUF") for q in range(2)], 15,
                                  i=pre[1])
                tick()
                g = next_ps()
                proj_chunk(wv, wk, 384, 16, xn_rhs, xn_keys, T, g)

                def emp(dst, lo, hi, smp, g=g):
                    ACT(dst, v3(PSG[g][:, lo:hi], smp), AF.Copy, [], ["PS%d" % g, "PBUF"])
                split_evac(PBUF, "PBUF", 15, TP, NS, emp)
                src, sk = PBUF, "PBUF"
                for i_ in range(c + 1):
                    d = 2 ** i_
                    lo = 2 ** (i_ + 1) - 1
                    dst, dk = TMPS[4 + i_ % 2], "TMP%d" % (4 + i_ % 2)
                    TT(dst[:, lo:Lp], src[:, lo:Lp], src[:, lo - d:Lp - d], ALU.add, [sk], [dk])
                    src, sk = dst, dk
                pmb, pk = PMB[c % 2], "PMB%d" % (c % 2)
                STT(pmb[:, 0:TP], src[:, 15:15 + TP], 1.0 / win, PBUF[:, 15:15 + TP], ALU.mult, ALU.subtract,
                    [sk, "PBUF"], [pk])
                if NS:
                    sv = src[:, 15 + TP:15 + TP + NS * 19].rearrange("p (s l) -> p s l", l=19)
                    STT(pmb[:, TP:T].rearrange("p (s l) -> p s l", l=4), sv[:, :, 15:19], 1.0 / win, pv[:, :, 15:19],
                        ALU.mult, ALU.subtract, [sk, "PBUF"], [pk])
                if pi == 0:
                    TT(TMPS[3][:, 0:16], src[:, 15:31], INV[:, c, :], ALU.mult, [sk, "CST"], ["TMP3"])
                    TT(pmb[:, 0:16], TMPS[3][:, 0:16], PBUF[:, 15:31], ALU.subtract, ["TMP3", "PBUF"], [pk])

                def pool_mm(pmb=pmb, pk=pk, c=c, l=l):
                    g = next_ps()
                    for (a_, b_) in tiles(T):
                        MM(PSG[g][:, a_:b_], POOLW[:, 4 * l + c, :], pmb[:, a_:b_], True, True, [pk, "POOLW"],
                           ["PS%d" % g])
                    ACT(MIX[:, 12 + c, 0:T], PSG[g][:, 0:T], AF.Copy, [], ["PS%d" % g, "MIX%d" % (12 + c)],
                        scale=PRM[:, O_PSC + 4 * l + c:O_PSC + 4 * l + c + 1])
                later(pool_mm, 1)
                if not last:
                    CP(HISTD[:, l, c, :], PBUF[:, TP:TP + 15], ["PBUF"], ["HD%d" % l], eng="act")
                else:
                    CP(STGM[:, 0:15], PBUF[:, TP:TP + 15], ["PBUF"], ["STGM"], eng="act")
                    CP(STGM[:, 15:15 + 15 * NS].rearrange("p (s k) -> p s k", k=15), pv[:, :, 4:19], ["PBUF"],
                       ["STGM"], eng="act")
                    if prefetch_next:
                        pre_cd["cd"] = (rows_in_load([ss_in[l][0:2 * NS, (c + 1) * 128:(c + 2) * 128]], 2 * NS),
                                        rows_in_load([sp_in[l][q * 120:(q + 1) * 120, (c + 1) * 128:(c + 2) * 128]
                                                      for q in range(2)], 120))
                    later(lambda l=l, c=c: state_out(STGM, "STGM", 15 + 15 * NS, op_out[l], c * 128), 1)
                pump(npump)

            pre_cd = {}
            branch_a_pair(0)
            branch_cd(0, True, 4)
            branch_cd(1, True, 4)
            branch_cd(2, True, 4)
            branch_cd(3, False, 4)
            drain()
            branch_a_pair(1)

            def ln_a_gen(l=l):
                gcol, bcol = O_CLG + 4 * l, O_CLB + 4 * l
                g1, g2 = next_ps(), next_ps()
                for c in range(4):
                    cb_, ck_ = LNA_SQ[c % 2], "GLUB0"
                    ACT(cb_[:, 0:T], S2[:, c, 0:T], AF.Copy, ["S2_%d" % c], [ck_])
                    for (a_, b_) in tiles(T):
                        MM(PSG[g1][:, a_:b_], ONESB_H, cb_[:, a_:b_], c == 0, c == 3, [ck_], ["PS%d" % g1])
                for c in range(4):
                    sq, sk_ = LNA_SQ[c % 2], "GLUB0"
                    ACT(sq[:, 0:T], S2[:, c, 0:T], AF.Square, ["S2_%d" % c], [sk_])
                    for (a_, b_) in tiles(T):
                        MM(PSG[g2][:, a_:b_], ONESB_H, sq[:, a_:b_], c == 0, c == 3, [sk_], ["PS%d" % g2])
                ACT(MEAN[:, 0:T], PSG[g1][:, 0:T], AF.Copy, [], ["PS%d" % g1, "MEAN"])
                TT(LNA_T[:, 0:T], MEAN[:, 0:T], MEAN[:, 0:T], ALU.mult, ["MEAN"], ["GLUB1"])
                TT(LNA_T[:, 0:T], PSG[g2][:, 0:T], LNA_T[:, 0:T], ALU.subtract, [], ["PS%d" % g2, "GLUB1"])
                yield
                ACT(RSTD[:, 0:T], LNA_T[:, 0:T], AF.Sqrt, ["GLUB1"], ["RSTD"], bias=EPSB[:, 0:1], scale=1.0)
                RECIP(RSTD[:, 0:T], RSTD[:, 0:T], [], ["RSTD"])
                yield
                for c in range(4):
                    TT(S2[:, c, 0:T], S2[:, c, 0:T], MEAN[:, 0:T], ALU.subtract, ["MEAN"], ["S2_%d" % c])
                    TT(S2[:, c, 0:T], S2[:, c, 0:T], RSTD[:, 0:T], ALU.mult, ["RSTD"], ["S2_%d" % c])
                    ACT(MIX[:, c, 0:T], S2[:, c, 0:T], AF.Silu, ["S2_%d" % c], ["MIX%d" % c],
                        scale=PRM[:, gcol + c:gcol + c + 1], bias=PRM[:, bcol + c:bcol + c + 1])
                    yield
            bg.append(ln_a_gen())

            phase_switch(ARENA_KEYS_M2, ARENA_KEYS_M1)
            units = []
            for mb in range(4):
                for b in (1, 2, 3, 0):
                    units.append((mb, b))
            pend = None
            for ui, (mb, b) in enumerate(units):
                oi = ui % 4
                if ui == 3:
                    drain()
                    phase_switch(["MG%d" % m_ for m_ in range(16)])
                wv, wk = wload([(w_in[l][:, GATES + b * 2048 + mb * 512:GATES + b * 2048 + (mb + 1) * 512], 16, 512)])
                bi = cnt["brw"] % 3
                cnt["brw"] += 1
                bw, bk = BRW[bi], "BRW%d" % bi
                DMA("pool", "C" + bk,
                    [(bw, w_br[l][b * 512:(b + 1) * 512, mb * 512:(mb + 1) * 512].rearrange("(k p) n -> p k n", p=128))],
                    [], [bk])
                for mi in range(4):
                    m = mb * 4 + mi
                    tick()
                    gg = next_ps()
                    proj_chunk(wv, wk, mi * 128, 16, xn_rhs, xn_keys, T, gg)
                    si = cnt["alt"] % 2
                    cnt["alt"] += 1
                    ACT(SIG[si][:, 0:T], PSG[gg][:, 0:T], AF.Sigmoid, [], ["PS%d" % gg, "SIG%d" % si])
                    gbr = next_ps()
                    for kc in range(4):
                        for (a, b_) in tiles(T):
                            MM(PSG[gbr][:, a:b_], bw[:, kc, mi * 128:(mi + 1) * 128], MIX[:, b * 4 + kc, a:b_],
                               kc == 0, kc == 3, [bk, "MIX%d" % (b * 4 + kc)], ["PS%d" % gbr])
                    if oi == 0:
                        TT(ACC[mi][:, 0:T], PSG[gbr][:, 0:T], SIG[si][:, 0:T], ALU.mult, ["SIG%d" % si],
                           ["PS%d" % gbr, "ACC%d" % mi])
                    else:
                        TT(PROD[:, 0:T], PSG[gbr][:, 0:T], SIG[si][:, 0:T], ALU.mult, ["SIG%d" % si],
                           ["PS%d" % gbr, "PROD"])
                        if oi < 3:
                            TT(ACC[mi][:, 0:T], ACC[mi][:, 0:T], PROD[:, 0:T], ALU.add, ["PROD"], ["ACC%d" % mi])
                        else:
                            TT(MG[:, m, 0:T], ACC[mi][:, 0:T], PROD[:, 0:T], ALU.add, ["PROD", "ACC%d" % mi],
                               ["MG%d" % m])
                    pump(8)
            def hsf_group(pc, l=l):
                rows_in_multi([sf_in[l][0:2 * NS, (4 * pc + q) * 128:(4 * pc + q + 1) * 128] for q in range(4)],
                              2 * NS, [(HSF[:, 4 * pc + q, :, :], "HSF") for q in range(4)], 2)
            if NS:
                phase_switch(["HSF"], MIXK)
                for pc in range(22):
                    later(lambda pc=pc: hsf_group(pc), 1 + (pc * 16) // 22)
            for mb in range(4):
                wv, wk = wload([(w_o[l][:, mb * 512:(mb + 1) * 512], 16, 512)])
                for mi in range(4):
                    m = mb * 4 + mi
                    tick()
                    g = next_ps()
                    proj_chunk(wv, wk, mi * 128, 16, lambda k: MG[:, k, :], lambda k: ["MG%d" % k], T, g)
                    TT(X[:, m, 0:T], PSG[g][:, 0:T], X[:, m, 0:T], ALU.add, [], ["PS%d" % g, "X%d" % m])

            phase_switch(ARENA_KEYS_FFN, ARENA_KEYS_M2 + MIXK)
            rmsnorm(T, O_NF + 16 * l, lambda c: XN[:, c, 0:T], lambda c: "XN%d" % c, "ffn")
            post_ops = []
            for jb in range(11):
                for half in range(2):
                    col0 = half * 5632 + jb * 512
                    wv, wk = wload([(f_up[l][:, col0:col0 + 512], 16, 512)])
                    for ji in range(4):
                        ch = half * 44 + jb * 4 + ji
                        hb, hk = HB[ji % 2], "HB%d" % (ji % 2)
                        CP(hb[:, 0:2], HISTF[:, l, ch, :], ["HF%d" % l], [hk], eng="act")
                        if NS:
                            hv = hb[:, 2 + TP:2 + TP + NS * 6].rearrange("p (s l) -> p s l", l=6)
                            CP(hv[:, :, 0:2], HSF[:, ch, :, :], ["HSF"], [hk], eng="act")
                        tick()
                        g = next_ps()
                        proj_chunk(wv, wk, ji * 128, 16, xn_rhs, xn_keys, T, g)

                        def emh(dst, lo, hi, smp, g=g, hk=hk):
                            ACT(dst, v3(PSG[g][:, lo:hi], smp), AF.Copy, [], ["PS%d" % g, hk])
                        split_evac(hb, hk, 2, TP, NS, emh)
                        for fn in post_ops:
                            fn()
                        del post_ops[:]
                        wc = O_FDW + 264 * l + 3 * ch
                        cv, ck = CVB[ji % 2], "CVB%d" % (ji % 2)
                        conv(cv, ck, hb, hk, 3, 2, TP, NS, wc)
                        if half == 0:
                            post_ops.append(lambda cv=cv, ck=ck, ji=ji: ACT(SGB[:, ji, 0:T], cv[:, 0:T], AF.Silu, [ck],
                                                                            ["SGB%d" % ji]))
                        else:
                            TT(ACTB[:, ji, 0:T], SGB[:, ji, 0:T], cv[:, 0:T], ALU.mult, ["SGB%d" % ji, ck], ["ACTB"])
                        if not last:
                            post_ops.append(lambda hb=hb, hk=hk, ch=ch, l=l: CP(HISTF[:, l, ch, :], hb[:, TP:TP + 2], [hk],
                                                                                ["HF%d" % l], eng="act"))
                        else:
                            sg, sgk = STG[(2 * jb + half) % 2], "STG%d" % ((2 * jb + half) % 2)

                            def stg_copies(sg=sg, sgk=sgk, hb=hb, hk=hk, hv=hv, ji=ji):
                                CP(sg[:, ji, 0:2], hb[:, TP:TP + 2], [hk], [sgk], eng="act")
                                CP(sg[:, ji, 2:2 + 2 * NS].rearrange("p (s k) -> p s k", k=2), hv[:, :, 4:6], [hk],
                                   [sgk], eng="act")
                            post_ops.append(stg_copies)
                    if last:
                        chb = half * 44 + jb * 4
                        later(lambda sg=sg, sgk=sgk, chb=chb, l=l: rows_out_multi(
                            [(sg[:, q, 0:2 + 2 * NS], sgk, 2 + 2 * NS) for q in range(4)],
                            [of_out[l][:, (chb + q) * 128:(chb + q + 1) * 128] for q in range(4)]), 3)
                def down(jb=jb, l=l):
                    wv, wk = wload([(f_dn[l][jb * 512:(jb + 1) * 512, :], 4, 2048)])
                    for m in range(16):
                        g = next_ps()
                        for k in range(4):
                            for (a, b) in tiles(T):
                                MM(PSG[g][:, a:b], wv[:, k, m * 128:(m + 1) * 128], ACTB[:, k, a:b], k == 0, k == 3,
                                   [wk, "ACTB"], ["PS%d" % g])
                        TT(X[:, m, 0:T], PSG[g][:, 0:T], X[:, m, 0:T], ALU.add, [], ["PS%d" % g, "X%d" % m])
                later(down, 3)
            for fn in post_ops:
                fn()
            del post_ops[:]
            flush()

        if not last:
            ng = [(r0, nr, col, fb) for (r0, nr, col) in row_tiles_of(pi + 1) for fb in range(4)]
            xpre[pi + 1] = [xload_issue(gp, True) for gp in ng[:3]]
        for fb in range(4):
            if fb == 0:
                g = next_ps()
                for c in range(16):
                    ti = c % 2
                    fb_ = FT[ti].bitcast(BF16)
                    ACT(fb_[:, 0:T], X[:, c, 0:T], AF.Square, ["X%d" % c], ["FT%d" % ti])
                    for (a, b) in tiles(T):
                        MM(PSG[g][:, a:b], ONESA_H, fb_[:, a:b], c == 0, c == 15, ["FT%d" % ti], ["PS%d" % g])
                ACT(RSTD[:, 0:T], PSG[g][:, 0:T], AF.Sqrt, [], ["PS%d" % g, "RSTD"], bias=EPSB[:, 0:1], scale=1.0)
                RECIP(RSTD[:, 0:T], RSTD[:, 0:T], [], ["RSTD"])
            for c in range(4):
                cc = fb * 4 + c
                STT(YN[:, c, 0:T], X[:, cc, 0:T], PRM[:, O_NFIN + cc:O_NFIN + cc + 1], RSTD[:, 0:T], ALU.mult,
                    ALU.mult, ["X%d" % cc, "RSTD"], ["YN%d" % c])
            for (r0, nr, col) in row_tiles:
                g = next_ps()
                for c in range(4):
                    TRP(PSG[g][0:nr, c * 128:(c + 1) * 128], YN[:, c, col:col + nr], IDENT, ["YN%d" % c],
                        ["PS%d" % g])
                i = cnt["yob"] % 4
                cnt["yob"] += 1
                CP(YOB[i][0:nr, 0:512], PSG[g][0:nr, 0:512], [], ["PS%d" % g, "YOB%d" % i], eng=alt_eng())
                DMA("sp", "CYOB%d" % i, [(y_out[r0:r0 + nr, fb * 512:(fb + 1) * 512], YOB[i][0:nr, 0:512])],
                    ["YOB%d" % i], [])

    tr.finalize()
    sem_names = ["E" + e for e in ENGS] + sorted(tr.chan_cnt.keys())
    final_waits = [(ch, 16 * n) for ch, n in tr.chan_cnt.items()]
    import contextlib
    with contextlib.ExitStack() as es:
        sems = {n: es.enter_context(nc.semaphore(n)) for n in sem_names}
        block = es.enter_context(nc.Block())

        @block.tensor
        def _(e):
            tr.emit("pe", e, sems)

        @block.scalar
        def _(e):
            tr.emit("act", e, sems)

        @block.vector
        def _(e):
            tr.emit("dve", e, sems)

        @block.gpsimd
        def _(e):
            tr.emit("pool", e, sems)

        @block.sync
        def _(e):
            tr.emit("sp", e, sems)
            for ch, v in final_waits:
                e.wait_ge(sems[ch], v)
    n_ins = {e: len(tr.streams[e]) for e in ENGS}
    print("kernel: ops per engine", n_ins, "arena words", m1_end, m2_end, ffn_end, "of", NW, flush=True)
    return nc


_NC_CACHE = {}


def _fm(v, n):
    return np.ascontiguousarray(v.reshape(n, 128).T)


def kernel(**inp):
    f = lambda k: np.asarray(inp[k], dtype=np.float32)
    x_prompt, x_sample = f("x_prompt"), f("x_sample")
    st_c, st_s, st_p, st_f = f("state_conf_conv"), f("state_sconv"), f("state_pool"), f("state_ffn_conv")
    prm = np.zeros((128, NPRM), np.float32)
    for l in range(2):
        prm[:, O_NM + 16 * l:O_NM + 16 * l + 16] = _fm(f("norm_mix")[l], 16)
        prm[:, O_NF + 16 * l:O_NF + 16 * l + 16] = _fm(f("norm_ffn")[l], 16)
        prm[:, O_CDW + 124 * l:O_CDW + 124 * (l + 1)] = f("conf_dw")[l].reshape(31, 4, 128).transpose(2, 1, 0).reshape(128, 124)
        prm[:, O_CLG + 4 * l:O_CLG + 4 * l + 4] = _fm(f("conf_ln_g")[l], 4)
        prm[:, O_CLB + 4 * l:O_CLB + 4 * l + 4] = _fm(f("conf_ln_b")[l], 4)
        prm[:, O_GLG + 4 * l:O_GLG + 4 * l + 4] = _fm(f("gmlp_ln_g")[l], 4)
        prm[:, O_GLB + 4 * l:O_GLB + 4 * l + 4] = _fm(f("gmlp_ln_b")[l], 4)
        prm[:, O_SDW + 12 * l:O_SDW + 12 * (l + 1)] = f("sconv_dw")[l].reshape(3, 4, 128).transpose(2, 1, 0).reshape(128, 12)
        prm[:, O_PSC + 4 * l:O_PSC + 4 * l + 4] = _fm(f("pool_scale")[l], 4)
        prm[:, O_FDW + 264 * l:O_FDW + 264 * (l + 1)] = f("ffn_dw")[l].reshape(3, 88, 128).transpose(2, 1, 0).reshape(128, 264)
    prm[:, O_NFIN:O_NFIN + 16] = _fm(f("norm_final"), 16)
    cst = np.zeros((128, 320), np.float32)
    cst[:, 0:128] = np.eye(128, dtype=np.float32)
    cst[:, 128:256] = np.triu(np.ones((128, 128), np.float32))
    for g, win in enumerate((2, 4, 8, 16)):
        cst[:, 256 + 16 * g:256 + 16 * (g + 1)] = 1.0 / np.minimum(win, np.arange(16) + 1.0)
    ws, gbias = f("gmlp_ws"), f("gmlp_b")
    wst = np.ascontiguousarray(ws.transpose(3, 0, 1, 2).reshape(128, 1024))
    gbb = np.ascontiguousarray(np.broadcast_to(gbias.reshape(1, 1024), (128, 1024)))
    sw = np.ascontiguousarray(np.broadcast_to(
        np.concatenate([ws[:, :, :4, :4].reshape(1, 128), gbias[:, :, :4].reshape(1, 32)], axis=1), (128, 160)))
    pw = np.ascontiguousarray(f("pool_w").transpose(2, 0, 1, 3).reshape(128, 1024))
    shared = dict(w_in=f("w_in"), w_br=np.ascontiguousarray(f("w_branch").reshape(2, 2048, 2048)), w_o=f("w_o"),
                  f_up=f("ffn_up"), f_dn=f("ffn_down"), prm=prm, cst=cst, wst=wst, gbb=gbb, sw=sw, pw=pw)
    in_maps = []
    for core in range(NCORE):
        sq, half = core // 2, core % 2
        w0 = 0 if half == 0 else 896
        sl = slice(core * NSQ, (core + 1) * NSQ)
        xw = np.concatenate([x_prompt[sq, w0:w0 + TPW], x_sample[sl].reshape(NSQ * 4, 2048)], axis=0)
        m = dict(shared)
        m.update(xw=np.ascontiguousarray(xw),
                 sc=np.ascontiguousarray(st_c[:, sl].reshape(2, NSQ * 30, 512)),
                 ss=np.ascontiguousarray(st_s[:, sl].reshape(2, NSQ * 2, 512)),
                 sp=np.ascontiguousarray(st_p[:, sl].reshape(2, NSQ * 15, 512)),
                 sf=np.ascontiguousarray(st_f[:, sl].reshape(2, NSQ * 2, 11264)))
        in_maps.append(m)
    if "nc" not in _NC_CACHE:
        _NC_CACHE["nc"] = build_nc()
    res = run_bass_kernel_spmd(_NC_CACHE["nc"], in_maps, core_ids=list(range(NCORE)))
    R = res.results
    y_prompt = np.zeros((4, 2048, 2048), np.float32)
    y_sample = np.zeros((128, 4, 2048), np.float32)
    conf_p = np.zeros((2, 4, 30, 512), np.float32)
    conf_s = np.zeros((2, 128, 30, 512), np.float32)
    sconv_p = np.zeros((2, 4, 2, 512), np.float32)
    sconv_s = np.zeros((2, 128, 2, 512), np.float32)
    pool_p = np.zeros((2, 4, 15, 512), np.float32)
    pool_s = np.zeros((2, 128, 15, 512), np.float32)
    ffn_p = np.zeros((2, 4, 2, 11264), np.float32)
    ffn_s = np.zeros((2, 128, 2, 11264), np.float32)
    v_s = np.zeros((2, 128, 4, 512), np.float32)
    for core in range(NCORE):
        sq, half = core // 2, core % 2
        sl = slice(core * NSQ, (core + 1) * NSQ)
        r = R[core]
        if half == 0:
            y_prompt[sq, 0:1026] = r["y"][0:1026]
        else:
            y_prompt[sq, 1026:2048] = r["y"][130:1152]
            conf_p[:, sq] = r["oc"][:, 0:30]
            sconv_p[:, sq] = r["os"][:, 0:2]
            pool_p[:, sq] = r["op"][:, 0:15]
            ffn_p[:, sq] = r["of"][:, 0:2]
        y_sample[sl] = r["y"][1152:1216].reshape(NSQ, 4, 2048)
        conf_s[:, sl] = r["oc"][:, 30:].reshape(2, NSQ, 30, 512)
        sconv_s[:, sl] = r["os"][:, 2:].reshape(2, NSQ, 2, 512)
        pool_s[:, sl] = r["op"][:, 15:].reshape(2, NSQ, 15, 512)
        ffn_s[:, sl] = r["of"][:, 2:].reshape(2, NSQ, 2, 11264)
        v_s[:, sl] = r["ov"].reshape(2, NSQ, 4, 512)
    return (y_prompt, y_sample, conf_p, conf_s, sconv_p, sconv_s, pool_p, pool_s, ffn_p, ffn_s, v_s)
```

```python
import numpy as np
import concourse.bass as bass
import concourse.mybir as mybir
from concourse.bass_utils import run_bass_kernel_spmd

F32 = mybir.dt.float32
BF16 = mybir.dt.bfloat16
AF = mybir.ActivationFunctionType
ALU = mybir.AluOpType
EPS = 1e-6
NCORE = 8
TPW = 1152
NSQ = 16
PASSES = [(640, 0, 0), (512, 16, 640)]
ENGS = ("pe", "act", "dve", "pool", "sp")
A_IN, A_GATE, G_U, G_V, C_B, C_C, C_H, P_IN, GATES = 0, 512, 1024, 1536, 2048, 2560, 3072, 3584, 4096
O_NM, O_NF, O_NFIN, O_CDW, O_CLG, O_CLB, O_GLG, O_GLB, O_SDW, O_PSC, O_FDW, NPRM = (
    0, 32, 64, 80, 328, 336, 344, 352, 360, 384, 392, 920)


class _Op:
    __slots__ = ("eng", "fn", "deps", "dma", "ndma", "sig", "val", "pos")


class Tracker:
    def __init__(self):
        self.streams = {e: [] for e in ENGS}
        self.wr = {}
        self.acc = {}
        self.chan_cnt = {}
        self.chan_last = {}
        self.npos = 0

    def op(self, eng, fn, r=(), w=(), dma=None, ndma=1):
        o = _Op()
        o.eng, o.fn, o.dma, o.ndma, o.sig, o.val = eng, fn, dma, ndma, False, 0
        deps = []
        for k in r:
            deps += self.wr.get(k, [])
        for k in w:
            deps += self.wr.get(k, [])
            deps += list(self.acc.get(k, {}).values())
        sid = dma if dma else eng
        for k in r:
            self.acc.setdefault(k, {})[sid] = o
        for k in w:
            self.acc[k] = {sid: o}
            self.wr[k] = [o]
        if dma:
            if dma in self.chan_last:
                deps.append(self.chan_last[dma])
            self.chan_last[dma] = o
            self.chan_cnt[dma] = self.chan_cnt.get(dma, 0) + ndma
            o.val = 16 * self.chan_cnt[dma]
        o.deps = deps
        self.npos += 1
        o.pos = self.npos
        self.streams[eng].append(o)
        return o

    def finalize(self):
        for e in ENGS:
            for o in self.streams[e]:
                for d in o.deps:
                    if d.dma is None and not (d.eng == "pe" and o.eng == "pe" and o.dma is None):
                        d.sig = True
        for e in ENGS:
            c = 0
            for o in self.streams[e]:
                if o.dma is None and o.sig:
                    c += 1
                    o.val = c

    def emit(self, eng_name, e, sems):
        waited = {}
        for o in self.streams[eng_name]:
            need = {}
            for d in o.deps:
                if d.dma is None and d.eng == "pe" and eng_name == "pe" and o.dma is None:
                    continue
                sn = d.dma if d.dma else "E" + d.eng
                if d.val > need.get(sn, 0):
                    need[sn] = d.val
            for sn, v in need.items():
                if waited.get(sn, 0) < v:
                    e.wait_ge(sems[sn], v)
                    waited[sn] = v
            ins = o.fn(e)
            if o.dma:
                for i in ins:
                    i.then_inc(sems[o.dma], 16)
            elif o.sig:
                ins.then_inc(sems["E" + eng_name], 1)


def build_nc():
    nc = bass.Bass("TRN2", target_bir_lowering=False)
    tr = Tracker()

    def din(name, shape):
        return nc.dram_tensor(name, shape, F32, kind="ExternalInput").ap()

    def dout(name, shape):
        return nc.dram_tensor(name, shape, F32, kind="ExternalOutput").ap()

    xw = din("xw", [1216, 2048])
    sc_in = din("sc", [2, 480, 512])
    ss_in = din("ss", [2, 32, 512])
    sp_in = din("sp", [2, 240, 512])
    sf_in = din("sf", [2, 32, 11264])
    w_in = din("w_in", [2, 2048, 12288])
    w_br = din("w_br", [2, 2048, 2048])
    w_o = din("w_o", [2, 2048, 2048])
    f_up = din("f_up", [2, 2048, 11264])
    f_dn = din("f_dn", [2, 5632, 2048])
    prm_in = din("prm", [128, NPRM])
    cst_in = din("cst", [128, 320])
    wst_in = din("wst", [128, 1024])
    gbb_in = din("gbb", [128, 1024])
    sw_in = din("sw", [128, 160])
    pw_in = din("pw", [128, 1024])
    y_out = dout("y", [1216, 2048])
    oc_out = dout("oc", [2, 510, 512])
    os_out = dout("os", [2, 34, 512])
    op_out = dout("op", [2, 255, 512])
    of_out = dout("of", [2, 34, 11264])
    ov_out = dout("ov", [2, 64, 512])

    NW = 53200
    AR = nc.alloc_sbuf_tensor("AR", [128, NW], F32)
    PSt = nc.alloc_psum_tensor("PS", [128, 8, 512], F32)
    PSG = [PSt[:, 2 * g:2 * g + 2, :].rearrange("p a b -> p (a b)") for g in range(4)]

    class Mem:
        def __init__(self, base):
            self.p = base

        def f(self, n):
            o = self.p
            self.p += n
            assert self.p <= NW, ("SBUF overflow", self.p)
            return o

    def fv(off, n):
        return AR[:, off:off + n]

    def bv(off, n):
        return AR[:, off:off + (n + 1) // 2].bitcast(BF16)

    pm = Mem(0)
    X = fv(pm.f(16 * 640), 16 * 640).rearrange("p (c t) -> p c t", c=16)
    XN = bv(pm.f(8 * 640), 16 * 640).rearrange("p (c t) -> p c t", c=16)
    WR = [bv(pm.f(4096), 8192) for _ in range(3)]
    PRM = fv(pm.f(NPRM), NPRM)
    CST = fv(pm.f(320), 320)
    IDENT = CST[:, 0:128]
    MASK = CST[:, 128:256]
    INV = CST[:, 256:320].rearrange("p (g j) -> p g j", g=4)
    _oh = pm.f(128)
    OH = AR[:, _oh:_oh + 128].bitcast(BF16)
    ONESA_H = OH[:, 0:128]
    ONESB_H = OH[:, 128:256]
    ONESB = fv(pm.f(128), 128)
    WT = bv(pm.f(512), 1024).rearrange("p (i t) -> p i t", i=8)
    POOLW = bv(pm.f(512), 1024).rearrange("p (i t) -> p i t", i=8)
    GB = fv(pm.f(1024), 1024).rearrange("p (i t) -> p i t", i=8)
    SW = fv(pm.f(160), 160)
    HISTA = fv(pm.f(240), 240).rearrange("p (l c k) -> p l c k", l=2, c=4)
    HISTC = fv(pm.f(16), 16).rearrange("p (l c k) -> p l c k", l=2, c=4)
    HISTD = fv(pm.f(120), 120).rearrange("p (l c k) -> p l c k", l=2, c=4)
    HISTF = fv(pm.f(352), 352).rearrange("p (l c k) -> p l c k", l=2, c=88)
    MEAN = fv(pm.f(640), 640)
    RSTD = fv(pm.f(640), 640)
    EPSB = CST[:, 128:136]
    IOB = [fv(pm.f(512), 512) for _ in range(4)]
    ARENA = pm.p

    REG = {}

    def areg(key, off, n):
        REG[key] = (off, off + n)
        return off

    am = Mem(ARENA)
    _o = am.f(8 * 640)
    MIX = bv(_o, 16 * 640).rearrange("p (c t) -> p c t", c=16)
    for i_ in range(16):
        areg("MIX%d" % i_, _o + 320 * i_, 320)
    after_mix = am.p
    _o = am.f(4 * 640)
    S2 = fv(_o, 4 * 640).rearrange("p (c t) -> p c t", c=4)
    for i_ in range(4):
        areg("S2_%d" % i_, _o + 640 * i_, 640)
    GLUB = [fv(areg("GLUB%d" % i_, am.f(1086), 1086), 1086) for i_ in range(2)]
    PBUF = fv(areg("PBUF", am.f(831), 831), 831)
    SCBUF = PBUF[:, 0:642]
    PMB = [bv(areg("PMB%d" % i_, am.f(320), 320), 640) for i_ in range(2)]
    TMPS = [fv(areg("TMP%d" % i_, am.f(832), 832), 832) for i_ in range(6)]
    _vtb = am.f(640)
    areg("VTB0", _vtb, 320)
    areg("VTB1", _vtb + 320, 320)
    VTB = [bv(_vtb, 640), bv(_vtb + 320, 640)]
    STGM = fv(areg("STGM", am.f(510), 510), 510)
    m1_end = am.p
    LNA_SQ = [bv(REG["GLUB0"][0], 640), bv(REG["GLUB0"][0] + 320, 640)]
    LNA_T = fv(REG["GLUB1"][0], 640)
    am = Mem(after_mix)
    _o = am.f(8 * 640)
    MG = bv(_o, 16 * 640).rearrange("p (c t) -> p c t", c=16)
    for i_ in range(16):
        areg("MG%d" % i_, _o + 320 * i_, 320)
    SIG = [fv(areg("SIG%d" % i_, am.f(640), 640), 640) for i_ in range(2)]
    PROD = fv(areg("PROD", am.f(640), 640), 640)
    ACC = [fv(areg("ACC%d" % i_, am.f(640), 640), 640) for i_ in range(4)]
    BRW = [bv(areg("BRW%d" % i_, am.f(1024), 1024), 2048).rearrange("p (k n) -> p k n", k=4) for i_ in range(3)]
    m2_end = am.p
    am = Mem(ARENA)
    HSF = fv(areg("HSF", am.f(88 * 32), 88 * 32), 88 * 32).rearrange("p (c s k) -> p c s k", c=88, s=16)
    _o = am.f(4 * 640)
    SGB = fv(_o, 4 * 640).rearrange("p (c t) -> p c t", c=4)
    for i_ in range(4):
        areg("SGB%d" % i_, _o + 640 * i_, 640)
    ACTB = bv(areg("ACTB", am.f(2 * 640), 2 * 640), 4 * 640).rearrange("p (c t) -> p c t", c=4)
    HB = [fv(areg("HB%d" % i_, am.f(642), 642), 642) for i_ in range(2)]
    CVB = [fv(areg("CVB%d" % i_, am.f(640), 640), 640) for i_ in range(2)]
    FT = [fv(areg("FT%d" % i_, am.f(640), 640), 640) for i_ in range(2)]
    STG = [fv(areg("STG%d" % i_, am.f(4 * 34), 4 * 34), 4 * 34).rearrange("p (c k) -> p c k", c=4) for i_ in range(2)]
    _o = am.f(4 * 640)
    YN = fv(_o, 4 * 640).rearrange("p (c t) -> p c t", c=4)
    for i_ in range(4):
        areg("YN%d" % i_, _o + 640 * i_, 640)
    YOB = [fv(areg("YOB%d" % i_, am.f(512), 512), 512) for i_ in range(4)]
    ffn_end = am.p
    ARENA_KEYS_M1 = (["S2_%d" % c for c in range(4)] + ["GLUB0", "GLUB1", "PBUF", "PMB0", "PMB1",
                     "VTB0", "VTB1", "STGM"] + ["TMP%d" % i for i in range(6)])
    ARENA_KEYS_M2 = (["MG%d" % c for c in range(16)] + ["SIG0", "SIG1", "PROD"] + ["ACC%d" % i for i in range(4)]
                     + ["BRW%d" % i for i in range(3)])
    ARENA_KEYS_FFN = (["HSF", "SGB0", "SGB1", "SGB2", "SGB3", "ACTB", "HB0", "HB1", "CVB0", "CVB1", "FT0", "FT1",
                       "STG0", "STG1"] + ["YN%d" % c for c in range(4)] + ["YOB%d" % i for i in range(4)])
    MIXK = ["MIX%d" % i for i in range(16)]
    ALL_ARENA_KEYS = ARENA_KEYS_M1 + ARENA_KEYS_M2 + ARENA_KEYS_FFN + MIXK

    def phase_switch(new_keys, old_keys=None):
        flush()
        upd = {}
        for nk in new_keys:
            lo, hi = REG[nk]
            best = {}
            for k in ALL_ARENA_KEYS:
                klo, khi = REG[k]
                if klo < hi and lo < khi:
                    for o in tr.wr.get(k, []) + list(tr.acc.get(k, {}).values()):
                        sid = o.dma if o.dma else o.eng
                        if sid not in best or o.pos > best[sid].pos:
                            best[sid] = o
            upd[nk] = best
        for nk, best in upd.items():
            tr.wr[nk] = list(best.values())
            tr.acc[nk] = dict(best)

    def ACT(out, in_, func, r, w, **kw):
        tr.op("act", lambda e: e.activation(out=out, in_=in_, func=func, **kw), r, w)

    def TT(out, in0, in1, op, r, w, eng="dve"):
        tr.op(eng, lambda e: e.tensor_tensor(out=out, in0=in0, in1=in1, op=op), r, w)

    def TS(out, in0, s1, s2, op0, op1, r, w, eng="dve"):
        if s2 is None:
            tr.op(eng, lambda e: e.tensor_scalar(out=out, in0=in0, scalar1=s1, scalar2=None, op0=op0), r, w)
        else:
            tr.op(eng, lambda e: e.tensor_scalar(out=out, in0=in0, scalar1=s1, scalar2=s2, op0=op0, op1=op1), r, w)

    def STT(out, in0, scalar, in1, op0, op1, r, w, eng="dve"):
        tr.op(eng, lambda e: e.scalar_tensor_tensor(out=out, in0=in0, scalar=scalar, in1=in1, op0=op0, op1=op1),
              r, w)

    def CP(out, in_, r, w, eng="dve"):
        if eng == "act":
            ACT(out, in_, AF.Copy, r, w)
        else:
            tr.op(eng, lambda e: e.tensor_copy(out=out, in_=in_), r, w)

    def MSET(ap, val, w, eng="dve"):
        tr.op(eng, lambda e: e.memset(ap, val), (), w)

    def RECIP(out, in_, r, w):
        tr.op("dve", lambda e: e.reciprocal(out=out, in_=in_), r, w)

    def MM(out, lhsT, rhs, start, stop, r, w):
        tr.op("pe", lambda e: e.matmul(out, lhsT, rhs, start=start, stop=stop), r, w)

    def TRP(out, in_, ident, r, w):
        tr.op("pe", lambda e: e.transpose(out, in_, ident), r, w)

    def DMA(eng, chan, pairs, r, w):
        tr.op(eng, lambda e: [e.dma_start(out=o, in_=i) for (o, i) in pairs], r, w, dma=chan, ndma=len(pairs))

    cnt = {"ps": 0, "wr": 0, "iob": 0, "brw": 0, "alt": 0, "yob": 0}

    def next_ps():
        g = cnt["ps"] % 4
        cnt["ps"] += 1
        return g

    iob_held = set()

    def next_iob():
        for _ in range(4):
            i = cnt["iob"] % 4
            cnt["iob"] += 1
            if i not in iob_held:
                return i
        raise RuntimeError("all IOB slots held")

    def alt_eng():
        cnt["alt"] += 1
        return "act" if cnt["alt"] % 2 else "dve"

    def wload(src_list):
        s = cnt["wr"] % 3
        cnt["wr"] += 1
        key = "WR%d" % s
        pairs = []
        k = src_list[0][1]
        ntot = sum(n for (_, _, n) in src_list)
        view = WR[s][:, 0:k * ntot].rearrange("p (k n) -> p k n", k=k)
        off = 0
        for (src, kk, n) in src_list:
            pairs.append((view[:, :, off:off + n], src.rearrange("(k p) n -> p k n", p=128)))
            off += n
        DMA("pool", "C" + key, pairs, (), [key])
        return view, key

    def tiles(T):
        return [(0, 512), (512, T)] if T > 512 else [(0, T)]

    deferred = []

    def later(fn, delay):
        deferred.append([delay, fn])

    def tick():
        for d in deferred:
            d[0] -= 1
        due = [d for d in deferred if d[0] <= 0]
        for d in due:
            deferred.remove(d)
            d[1]()

    def flush():
        while deferred:
            d = deferred.pop(0)
            d[1]()

    def proj_chunk(wview, wkey, mcol, nk, rhs_fn, rkeys, T, g):
        for k in range(nk):
            for (a, b) in tiles(T):
                MM(PSG[g][:, a:b], wview[:, k, mcol:mcol + 128], rhs_fn(k)[:, a:b], k == 0, k == nk - 1,
                   [wkey] + rkeys(k), ["PS%d" % g])

    def stats_sum(src_aps, skeys, ones, T, g, square, tmpi):
        n = len(src_aps)
        for c in range(n):
            if square:
                ti = tmpi[c % 2]
                sqb = TMPS[ti].bitcast(BF16)
                ACT(sqb[:, 0:T], src_aps[c], AF.Square, [skeys[c]], ["TMP%d" % ti])
                rhs, rk = sqb, "TMP%d" % ti
            elif tmpi is not None:
                ti = tmpi[c % 2]
                cpb = TMPS[ti].bitcast(BF16)
                ACT(cpb[:, 0:T], src_aps[c], AF.Copy, [skeys[c]], ["TMP%d" % ti])
                rhs, rk = cpb, "TMP%d" % ti
            else:
                rhs, rk = src_aps[c], skeys[c]
            for (a, b) in tiles(T):
                MM(PSG[g][:, a:b], ones, rhs[:, a:b], c == 0, c == n - 1, [rk], ["PS%d" % g])

    def rmsnorm(T, gcol, out_fn, okey_fn, tmpk):
        g = next_ps()
        srcs = [X[:, c, 0:T] for c in range(16)]
        keys = ["X%d" % c for c in range(16)]
        if tmpk == "ffn":
            for c in range(16):
                ti = c % 2
                fb_ = FT[ti].bitcast(BF16)
                ACT(fb_[:, 0:T], srcs[c], AF.Square, [keys[c]], ["FT%d" % ti])
                for (a, b) in tiles(T):
                    MM(PSG[g][:, a:b], ONESA_H, fb_[:, a:b], c == 0, c == 15, ["FT%d" % ti], ["PS%d" % g])
        else:
            stats_sum(srcs, keys, ONESA_H, T, g, True, (0, 1))
        ACT(RSTD[:, 0:T], PSG[g][:, 0:T], AF.Sqrt, [], ["PS%d" % g, "RSTD"], bias=EPSB[:, 0:1], scale=1.0)
        RECIP(RSTD[:, 0:T], RSTD[:, 0:T], [], ["RSTD"])
        for c in range(16):
            STT(out_fn(c), X[:, c, 0:T], PRM[:, gcol + c:gcol + c + 1], RSTD[:, 0:T], ALU.mult, ALU.mult,
                ["X%d" % c, "RSTD"], [okey_fn(c)])

    def layernorm4(T, gcol, bcol, func, out_fn, okey_fn):
        g1, g2 = next_ps(), next_ps()
        srcs = [S2[:, c, 0:T] for c in range(4)]
        keys = ["S2_%d" % c for c in range(4)]
        stats_sum(srcs, keys, ONESB_H, T, g1, False, (0, 1))
        stats_sum(srcs, keys, ONESB_H, T, g2, True, (0, 1))
        ACT(MEAN[:, 0:T], PSG[g1][:, 0:T], AF.Copy, [], ["PS%d" % g1, "MEAN"])
        TT(TMPS[2][:, 0:T], MEAN[:, 0:T], MEAN[:, 0:T], ALU.mult, ["MEAN"], ["TMP2"])
        TT(TMPS[3][:, 0:T], PSG[g2][:, 0:T], TMPS[2][:, 0:T], ALU.subtract, ["TMP2"], ["PS%d" % g2, "TMP3"])
        ACT(RSTD[:, 0:T], TMPS[3][:, 0:T], AF.Sqrt, ["TMP3"], ["RSTD"], bias=EPSB[:, 0:1], scale=1.0)
        RECIP(RSTD[:, 0:T], RSTD[:, 0:T], [], ["RSTD"])
        for c in range(4):
            ti = c % 2
            TT(TMPS[ti][:, 0:T], S2[:, c, 0:T], MEAN[:, 0:T], ALU.subtract, ["S2_%d" % c, "MEAN"], ["TMP%d" % ti])
            TT(TMPS[ti][:, 0:T], TMPS[ti][:, 0:T], RSTD[:, 0:T], ALU.mult, ["RSTD"], ["TMP%d" % ti])
            ACT(out_fn(c), TMPS[ti][:, 0:T], func, ["TMP%d" % ti], [okey_fn(c)],
                scale=PRM[:, gcol + c:gcol + c + 1], bias=PRM[:, bcol + c:bcol + c + 1])

    def conv(out, okey, buf, bkey, K, hist, TP, NS, wcol):
        L = hist + 4
        for k in range(K):
            wk = PRM[:, wcol + k:wcol + k + 1]
            srcs = [(out[:, 0:TP], buf[:, k:k + TP])]
            if NS:
                sb = buf[:, hist + TP:hist + TP + NS * L].rearrange("p (s l) -> p s l", l=L)
                srcs.append((out[:, TP:TP + 4 * NS].rearrange("p (s l) -> p s l", l=4), sb[:, :, k:k + 4]))
            for (o, i) in srcs:
                if k == 0:
                    TS(o, i, wk, None, ALU.mult, None, [bkey], [okey])
                else:
                    STT(o, i, wk, o, ALU.mult, ALU.add, [bkey], [okey])

    def split_evac(buf, bkey, hist, TP, NS, emit):
        emit(buf[:, hist:hist + TP], 0, TP, False)
        if NS:
            L = hist + 4
            sb = buf[:, hist + TP:hist + TP + NS * L].rearrange("p (s l) -> p s l", l=L)
            emit(sb[:, :, hist:hist + 4], TP, TP + 4 * NS, True)

    def v3(ap, is_sample):
        return ap.rearrange("p (s l) -> p s l", l=4) if is_sample else ap

    def rows_out_multi(items, dsts):
        g = next_ps()
        nmax = max(n for (_, _, n) in items)
        for j, (ap, key, n) in enumerate(items):
            TRP(PSG[g][0:n, j * 128:(j + 1) * 128], ap, IDENT, [key], ["PS%d" % g])
        i = next_iob()
        CP(IOB[i][0:nmax, 0:128 * len(items)], PSG[g][0:nmax, 0:128 * len(items)], [],
           ["PS%d" % g, "IOB%d" % i], eng=alt_eng())
        DMA("sp", "CIOB%d" % i, [(dsts[j], IOB[i][0:items[j][2], j * 128:(j + 1) * 128])
                                 for j in range(len(items))], ["IOB%d" % i], [])

    def rows_in_load(srcs, n):
        i = next_iob()
        iob_held.add(i)
        DMA("sp", "CIOB%d" % i, [(IOB[i][0:n, j * 128:(j + 1) * 128], srcs[j]) for j in range(len(srcs))], [],
            ["IOB%d" % i])
        return i

    def rows_in_multi(srcs, n, dsts, k, i=None):
        if i is None:
            i = rows_in_load(srcs, n)
        iob_held.discard(i)
        g = next_ps()
        for j in range(len(srcs)):
            TRP(PSG[g][:, j * 128:j * 128 + n], IOB[i][0:n, j * 128:(j + 1) * 128], IDENT[0:n, 0:n],
                ["IOB%d" % i], ["PS%d" % g])
        for j in range(len(srcs)):
            dst, dkey = dsts[j]
            CP(dst, PSG[g][:, j * 128:j * 128 + n].rearrange("p (s k) -> p s k", k=k), [],
               ["PS%d" % g, dkey], eng=alt_eng())

    def state_out(stg, skey, nrows, dst, ccol):
        r = 0
        items, dsts = [], []
        while r < nrows:
            n = min(120, nrows - r)
            items.append((stg[:, r:r + n], skey, n))
            dsts.append(dst[r:r + n, ccol:ccol + 128])
            r += n
            if len(items) == 4 or r >= nrows:
                rows_out_multi(items, dsts)
                items, dsts = [], []

    DMA("sp", "CSET", [(PRM, prm_in), (CST, cst_in), (GB.rearrange("p i t -> p (i t)"), gbb_in), (SW, sw_in)],
        [], ["PRM", "CST", "GB", "SW"])
    MSET(ONESA_H, 1.0 / 2048.0, ["ONESA"])
    MSET(ONESB_H, 1.0 / 512.0, ["ONESA"])
    MSET(ONESB, 1.0 / 512.0, ["ONESB"])
    MSET(HISTA.rearrange("p l c k -> p (l c k)"), 0.0, ["HA0", "HA1"])
    MSET(HISTC.rearrange("p l c k -> p (l c k)"), 0.0, ["HC0", "HC1"])
    MSET(HISTD.rearrange("p l c k -> p (l c k)"), 0.0, ["HD0", "HD1"])
    MSET(HISTF.rearrange("p l c k -> p (l c k)"), 0.0, ["HF0", "HF1"])
    DMA("sp", "CSET2", [(TMPS[0][:, 0:512], wst_in[:, 0:512]), (TMPS[1][:, 0:512], wst_in[:, 512:1024]),
                        (TMPS[2][:, 0:512], pw_in[:, 0:512]), (TMPS[3][:, 0:512], pw_in[:, 512:1024])],
        [], ["TMP0", "TMP1", "TMP2", "TMP3"])
    for i in range(8):
        TT(WT[:, i, :], TMPS[i // 4][:, (i % 4) * 128:(i % 4 + 1) * 128], MASK, ALU.mult,
           ["TMP%d" % (i // 4), "CST"], ["WT"])
    for i in range(2):
        CP(POOLW[:, 4 * i:4 * i + 4, :].rearrange("p i t -> p (i t)"), TMPS[2 + i][:, 0:512],
           ["TMP%d" % (2 + i)], ["POOLW"], eng="act")
    MSET(EPSB, EPS, ["EPSB", "CST"])

    SETUP_KEYS = ["PRM", "CST", "GB", "SW", "ONESA", "ONESB", "EPSB", "WT", "POOLW"]
    for en in ("pe", "act", "dve"):
        tr.op(en, lambda e: e.nop(), SETUP_KEYS, ())

    def row_tiles_of(pi):
        TP_, NS_, C0_ = PASSES[pi]
        rt = [(C0_ + r * 128, 128, r * 128) for r in range(TP_ // 128)]
        if NS_:
            rt.append((TPW, 4 * NS_, TP_))
        return rt

    def xload_issue(group, use_iob):
        r0, nr, col, fb = group
        if use_iob:
            i = next_iob()
            iob_held.add(i)
            buf, key, ch, hi = IOB[i], "IOB%d" % i, "CIOB%d" % i, i
        else:
            i = cnt["yob"] % 4
            cnt["yob"] += 1
            buf, key, ch, hi = YOB[i], "YOB%d" % i, "CYOB%d" % i, None
        DMA("sp", ch, [(buf[0:nr, 0:512], xw[r0:r0 + nr, fb * 512:(fb + 1) * 512])], [], [key])
        return (buf, key, hi)

    xpre = {}
    for pi, (TP, NS, C0) in enumerate(PASSES):
        T = TP + 4 * NS
        last = (pi == len(PASSES) - 1)
        phase_switch(ARENA_KEYS_FFN, ARENA_KEYS_M1)
        row_tiles = row_tiles_of(pi)
        groups = [(r0, nr, col, fb) for (r0, nr, col) in row_tiles for fb in range(4)]
        pre = xpre.pop(pi, [])
        for gi, (r0, nr, col, fb) in enumerate(groups):
            if gi < len(pre):
                buf, key, hi = pre[gi]
            else:
                buf, key, hi = xload_issue(groups[gi], (gi % 7) >= 4)
            if hi is not None:
                iob_held.discard(hi)
            g = next_ps()
            for c in range(4):
                TRP(PSG[g][:, c * 128:c * 128 + nr], buf[0:nr, c * 128:(c + 1) * 128], IDENT[0:nr, 0:nr],
                    [key], ["PS%d" % g])
            for c in range(4):
                CP(X[:, fb * 4 + c, col:col + nr], PSG[g][:, c * 128:c * 128 + nr], [],
                   ["PS%d" % g, "X%d" % (fb * 4 + c)], eng=alt_eng())

        for l in range(2):
            phase_switch(ARENA_KEYS_M1 + MIXK, ARENA_KEYS_FFN)
            def conf_hist_load(c, l=l):
                return rows_in_load([sc_in[l][q * 120:(q + 1) * 120, c * 128:(c + 1) * 128] for q in range(4)], 120)
            rmsnorm(T, O_NM + 16 * l, lambda c: XN[:, c, 0:T], lambda c: "XN%d" % c, "m1")
            xn_rhs = lambda k: XN[:, k, :]
            xn_keys = lambda k: ["XN%d" % k]

            def wcols(col, n=128, l=l):
                return (w_in[l][:, col:col + n], 16, n)

            for cp in range(2):
                c0, c1 = 2 * cp, 2 * cp + 1
                wv, wk = wload([wcols(G_U + c0 * 128), wcols(G_V + c0 * 128), wcols(G_U + c1 * 128),
                                wcols(G_V + c1 * 128)])
                for j, c in enumerate((c0, c1)):
                    tick()
                    g = next_ps()
                    proj_chunk(wv, wk, (2 * j) * 128, 16, xn_rhs, xn_keys, T, g)
                    ACT(MIX[:, 4 + c, 0:T], PSG[g][:, 0:T], AF.Gelu_apprx_tanh, [], ["PS%d" % g, "MIX%d" % (4 + c)])
                    tick()
                    g = next_ps()
                    proj_chunk(wv, wk, (2 * j + 1) * 128, 16, xn_rhs, xn_keys, T, g)
                    ACT(S2[:, c, 0:T], PSG[g][:, 0:T], AF.Gelu_apprx_tanh, [], ["PS%d" % g, "S2_%d" % c])
            layernorm4(T, O_GLG + 4 * l, O_GLB + 4 * l, AF.Identity, lambda c: S2[:, c, 0:T],
                       lambda c: "S2_%d" % c)
            if NS:
                rows_out_multi([(S2[:, c, TP:T], "S2_%d" % c, 4 * NS) for c in range(4)],
                               [ov_out[l][:, c * 128:(c + 1) * 128] for c in range(4)])
            def gmlp_heads(l=l):
                ntc = TP // 128
                for h in range(4):
                    gA, gM = next_ps(), next_ps()
                    vt, vk = VTB[h % 2], "VTB%d" % (h % 2)
                    for tc in range(ntc):
                        TRP(PSG[gA][:, tc * 128:(tc + 1) * 128], S2[:, h, tc * 128:(tc + 1) * 128], IDENT,
                            ["S2_%d" % h], ["PS%d" % gA])
                    CP(vt[:, 0:TP], PSG[gA][:, 0:TP], [], ["PS%d" % gA, vk], eng="act")
                    for tc in range(ntc):
                        MM(PSG[gM][:, tc * 128:(tc + 1) * 128], vt[:, tc * 128:(tc + 1) * 128], WT[:, 4 * l + h, :],
                           True, True, [vk, "WT"], ["PS%d" % gM])
                    ti = 4 + h % 2
                    for tc in range(ntc):
                        TT(TMPS[ti][:, tc * 128:(tc + 1) * 128], PSG[gM][:, tc * 128:(tc + 1) * 128], GB[:, 4 * l + h, :],
                           ALU.add, ["GB"], ["PS%d" % gM, "TMP%d" % ti])
                    TT(MIX[:, 4 + h, 0:TP], TMPS[ti][:, 0:TP], MIX[:, 4 + h, 0:TP], ALU.mult, ["TMP%d" % ti],
                       ["MIX%d" % (4 + h)])
                    if NS:
                        vs = S2[:, h, TP:T].rearrange("p (s l) -> p s l", l=4)
                        mx = TMPS[3][:, 0:4 * NS].rearrange("p (s l) -> p s l", l=4)
                        wb = (4 * l + h) * 16
                        bb = 128 + (4 * l + h) * 4
                        for t in range(4):
                            TS(mx[:, :, t], vs[:, :, 0], SW[:, wb + 4 * t:wb + 4 * t + 1], SW[:, bb + t:bb + t + 1],
                               ALU.mult, ALU.add, ["S2_%d" % h, "SW"], ["TMP3"])
                            for s_ in range(1, t + 1):
                                STT(mx[:, :, t], vs[:, :, s_], SW[:, wb + 4 * t + s_:wb + 4 * t + s_ + 1], mx[:, :, t],
                                    ALU.mult, ALU.add, ["S2_%d" % h, "SW"], ["TMP3"])
                        TT(MIX[:, 4 + h, TP:T], TMPS[3][:, 0:4 * NS], MIX[:, 4 + h, TP:T], ALU.mult, ["TMP3"],
                           ["MIX%d" % (4 + h)])


            bg = []

            def pump(n):
                while n > 0 and bg:
                    try:
                        next(bg[0])
                        n -= 1
                    except StopIteration:
                        bg.pop(0)

            def drain():
                pump(1 << 30)

            def conv_gen(out, okey, buf, bkey, K, hist, wcol):
                L = hist + 4
                for k in range(K):
                    wk_ = PRM[:, wcol + k:wcol + k + 1]
                    srcs = [(out[:, 0:TP], buf[:, k:k + TP])]
                    if NS:
                        sb = buf[:, hist + TP:hist + TP + NS * L].rearrange("p (s l) -> p s l", l=L)
                        srcs.append((out[:, TP:TP + 4 * NS].rearrange("p (s l) -> p s l", l=4), sb[:, :, k:k + 4]))
                    for (o, i) in srcs:
                        if k == 0:
                            TS(o, i, wk_, None, ALU.mult, None, [bkey], [okey])
                        else:
                            STT(o, i, wk_, o, ALU.mult, ALU.add, [bkey], [okey])
                    yield

            def branch_a_pair(cp, l=l):
                c0, c1 = 2 * cp, 2 * cp + 1
                wv, wk = wload([wcols(A_IN + c0 * 128), wcols(A_GATE + c0 * 128), wcols(A_IN + c1 * 128),
                                wcols(A_GATE + c1 * 128)])
                pre_a = {}
                if NS:
                    pre_a[c0] = conf_hist_load(c0)
                    pre_a[c1] = conf_hist_load(c1)
                for j, c in enumerate((c0, c1)):
                    gb, gk = GLUB[c % 2], "GLUB%d" % (c % 2)
                    CP(gb[:, 0:30], HISTA[:, l, c, :], ["HA%d" % l], [gk])
                    if NS:
                        sbv = gb[:, 30 + TP:30 + TP + NS * 34].rearrange("p (s l) -> p s l", l=34)
                        rows_in_multi([None] * 4, 120, [(sbv[:, 4 * q:4 * q + 4, 0:30], gk) for q in range(4)], 30,
                                      i=pre_a[c])
                    tick()
                    g = next_ps()
                    proj_chunk(wv, wk, (2 * j) * 128, 16, xn_rhs, xn_keys, T, g)
                    ta, tak = TMPS[c % 2], "TMP%d" % (c % 2)
                    ACT(ta[:, 0:T], PSG[g][:, 0:T], AF.Copy, [], ["PS%d" % g, tak])
                    tick()
                    g = next_ps()
                    proj_chunk(wv, wk, (2 * j + 1) * 128, 16, xn_rhs, xn_keys, T, g)
                    tb, tbk = TMPS[2 + c % 2], "TMP%d" % (2 + c % 2)
                    ACT(tb[:, 0:T], PSG[g][:, 0:T], AF.Sigmoid, [], ["PS%d" % g, tbk])

                    def em(dst, lo, hi, smp, ta=ta, tb=tb, tak=tak, tbk=tbk, gk=gk):
                        TT(dst, v3(ta[:, lo:hi], smp), v3(tb[:, lo:hi], smp), ALU.mult, [tak, tbk], [gk])
                    split_evac(gb, gk, 30, TP, NS, em)
                    if not last:
                        CP(HISTA[:, l, c, :], gb[:, TP:TP + 30], [gk], ["HA%d" % l], eng="act")
                    else:
                        CP(STGM[:, 0:30], gb[:, TP:TP + 30], [gk], ["STGM"], eng="act")
                        CP(STGM[:, 30:30 + 30 * NS].rearrange("p (s k) -> p s k", k=30), sbv[:, :, 4:34], [gk],
                           ["STGM"], eng="act")
                        state_out(STGM, "STGM", 30 + 30 * NS, oc_out[l], c * 128)
                if cp == 0:
                    gmlp_heads()
                for c in (c0, c1):
                    bg.append(conv_gen(S2[:, c, :], "S2_%d" % c, GLUB[c % 2], "GLUB%d" % (c % 2), 31, 30,
                                       O_CDW + 124 * l + 31 * c))

            def branch_cd(c, prefetch_next, npump, l=l):
                wv, wk = wload([wcols(C_C + c * 128), wcols(C_H + c * 128), wcols(C_B + c * 128),
                                wcols(P_IN + c * 128)])
                CP(SCBUF[:, 0:2], HISTC[:, l, c, :], ["HC%d" % l], ["PBUF"])
                if NS:
                    scv = SCBUF[:, 2 + TP:2 + TP + NS * 6].rearrange("p (s l) -> p s l", l=6)
                    pv = PBUF[:, 15 + TP:15 + TP + NS * 19].rearrange("p (s l) -> p s l", l=19)
                    if "cd" not in pre_cd:
                        pre_cd["cd"] = (rows_in_load([ss_in[l][0:2 * NS, c * 128:(c + 1) * 128]], 2 * NS),
                                        rows_in_load([sp_in[l][q * 120:(q + 1) * 120, c * 128:(c + 1) * 128]
                                                      for q in range(2)], 120))
                    pre = pre_cd.pop("cd")
                    rows_in_multi([None], 2 * NS, [(scv[:, :, 0:2], "PBUF")], 2, i=pre[0])
                tick()
                g = next_ps()
                proj_chunk(wv, wk, 0, 16, xn_rhs, xn_keys, T, g)
                ACT(TMPS[0][:, 0:T], PSG[g][:, 0:T], AF.Copy, [], ["PS%d" % g, "TMP0"])
                pump(npump)
                tick()
                g = next_ps()
                proj_chunk(wv, wk, 128, 16, xn_rhs, xn_keys, T, g)

                def emc(dst, lo, hi, smp, g=g):
                    TT(dst, v3(PSG[g][:, lo:hi], smp), v3(TMPS[0][:, lo:hi], smp), ALU.mult, ["TMP0"],
                       ["PS%d" % g, "PBUF"])
                split_evac(SCBUF, "PBUF", 2, TP, NS, emc)
                conv(TMPS[1], "TMP1", SCBUF, "PBUF", 3, 2, TP, NS, O_SDW + 12 * l + 3 * c)
                pump(npump)
                tick()
                g = next_ps()
                proj_chunk(wv, wk, 256, 16, xn_rhs, xn_keys, T, g)
                TT(MIX[:, 8 + c, 0:T], PSG[g][:, 0:T], TMPS[1][:, 0:T], ALU.mult, ["TMP1"],
                   ["PS%d" % g, "MIX%d" % (8 + c)])
                if not last:
                    CP(HISTC[:, l, c, :], SCBUF[:, TP:TP + 2], ["PBUF"], ["HC%d" % l], eng="act")
                else:
                    st = TMPS[2]
                    CP(st[:, 0:2], SCBUF[:, TP:TP + 2], ["PBUF"], ["TMP2"], eng="act")
                    CP(st[:, 2:2 + 2 * NS].rearrange("p (s k) -> p s k", k=2), scv[:, :, 4:6], ["PBUF"], ["TMP2"],
                       eng="act")
                    later(lambda st=st, l=l, c=c: state_out(st, "TMP2", 2 + 2 * NS, os_out[l], c * 128), 1)
                pump(npump)
                win = 2 ** (c + 1)
                Lp = 15 + TP + NS * 19
                CP(PBUF[:, 0:15], HISTD[:, l, c, :], ["HD%d" % l], ["PBUF"], eng="act")
                if NS:
                    rows_in_multi([None, None], 120, [(pv[:, 8 * q:8 * q + 8, 0:15], "PBUF") for q in range(2)], 15,
                                  i=pre[1])
                tick()
                g = next_ps()
                proj_chunk(wv, wk, 384, 16, xn_rhs, xn_keys, T, g)

                def emp(dst, lo, hi, smp, g=g):
                    ACT(dst, v3(PSG[g][:, lo:hi], smp), AF.Copy, [], ["PS%d" % g, "PBUF"])
                split_evac(PBUF, "PBUF", 15, TP, NS, emp)
                src, sk = PBUF, "PBUF"
                for i_ in range(c + 1):
                    d = 2 ** i_
                    lo = 2 ** (i_ + 1) - 1
                    dst, dk = TMPS[4 + i_ % 2], "TMP%d" % (4 + i_ % 2)
                    TT(dst[:, lo:Lp], src[:, lo:Lp], src[:, lo - d:Lp - d], ALU.add, [sk], [dk])
                    src, sk = dst, dk
                pmb, pk = PMB[c % 2], "PMB%d" % (c % 2)
                STT(pmb[:, 0:TP], src[:, 15:15 + TP], 1.0 / win, PBUF[:, 15:15 + TP], ALU.mult, ALU.subtract,
                    [sk, "PBUF"], [pk])
                if NS:
                    sv = src[:, 15 + TP:15 + TP + NS * 19].rearrange("p (s l) -> p s l", l=19)
                    STT(pmb[:, TP:T].rearrange("p (s l) -> p s l", l=4), sv[:, :, 15:19], 1.0 / win, pv[:, :, 15:19],
                        ALU.mult, ALU.subtract, [sk, "PBUF"], [pk])
                if pi == 0:
                    TT(TMPS[3][:, 0:16], src[:, 15:31], INV[:, c, :], ALU.mult, [sk, "CST"], ["TMP3"])
                    TT(pmb[:, 0:16], TMPS[3][:, 0:16], PBUF[:, 15:31], ALU.subtract, ["TMP3", "PBUF"], [pk])

                def pool_mm(pmb=pmb, pk=pk, c=c, l=l):
                    g = next_ps()
                    for (a_, b_) in tiles(T):
                        MM(PSG[g][:, a_:b_], POOLW[:, 4 * l + c, :], pmb[:, a_:b_], True, True, [pk, "POOLW"],
                           ["PS%d" % g])
                    ACT(MIX[:, 12 + c, 0:T], PSG[g][:, 0:T], AF.Copy, [], ["PS%d" % g, "MIX%d" % (12 + c)],
                        scale=PRM[:, O_PSC + 4 * l + c:O_PSC + 4 * l + c + 1])
                later(pool_mm, 1)
                if not last:
                    CP(HISTD[:, l, c, :], PBUF[:, TP:TP + 15], ["PBUF"], ["HD%d" % l], eng="act")
                else:
                    CP(STGM[:, 0:15], PBUF[:, TP:TP + 15], ["PBUF"], ["STGM"], eng="act")
                    CP(STGM[:, 15:15 + 15 * NS].rearrange("p (s k) -> p s k", k=15), pv[:, :, 4:19], ["PBUF"],
                       ["STGM"], eng="act")
                    if prefetch_next:
                        pre_cd["cd"] = (rows_in_load([ss_in[l][0:2 * NS, (c + 1) * 128:(c + 2) * 128]], 2 * NS),
                                        rows_in_load([sp_in[l][q * 120:(q + 1) * 120, (c + 1) * 128:(c + 2) * 128]
                                                      for q in range(2)], 120))
                    later(lambda l=l, c=c: state_out(STGM, "STGM", 15 + 15 * NS, op_out[l], c * 128), 1)
                pump(npump)

            pre_cd = {}
            branch_a_pair(0)
            branch_cd(0, True, 4)
            branch_cd(1, True, 4)
            branch_cd(2, True, 4)
            branch_cd(3, False, 4)
            drain()
            branch_a_pair(1)

            def ln_a_gen(l=l):
                gcol, bcol = O_CLG + 4 * l, O_CLB + 4 * l
                g1, g2 = next_ps(), next_ps()
                for c in range(4):
                    cb_, ck_ = LNA_SQ[c % 2], "GLUB0"
                    ACT(cb_[:, 0:T], S2[:, c, 0:T], AF.Copy, ["S2_%d" % c], [ck_])
                    for (a_, b_) in tiles(T):
                        MM(PSG[g1][:, a_:b_], ONESB_H, cb_[:, a_:b_], c == 0, c == 3, [ck_], ["PS%d" % g1])
                for c in range(4):
                    sq, sk_ = LNA_SQ[c % 2], "GLUB0"
                    ACT(sq[:, 0:T], S2[:, c, 0:T], AF.Square, ["S2_%d" % c], [sk_])
                    for (a_, b_) in tiles(T):
                        MM(PSG[g2][:, a_:b_], ONESB_H, sq[:, a_:b_], c == 0, c == 3, [sk_], ["PS%d" % g2])
                ACT(MEAN[:, 0:T], PSG[g1][:, 0:T], AF.Copy, [], ["PS%d" % g1, "MEAN"])
                TT(LNA_T[:, 0:T], MEAN[:, 0:T], MEAN[:, 0:T], ALU.mult, ["MEAN"], ["GLUB1"])
                TT(LNA_T[:, 0:T], PSG[g2][:, 0:T], LNA_T[:, 0:T], ALU.subtract, [], ["PS%d" % g2, "GLUB1"])
                yield
                ACT(RSTD[:, 0:T], LNA_T[:, 0:T], AF.Sqrt, ["GLUB1"], ["RSTD"], bias=EPSB[:, 0:1], scale=1.0)
                RECIP(RSTD[:, 0:T], RSTD[:, 0:T], [], ["RSTD"])
                yield
                for c in range(4):
                    TT(S2[:, c, 0:T], S2[:, c, 0:T], MEAN[:, 0:T], ALU.subtract, ["MEAN"], ["S2_%d" % c])
                    TT(S2[:, c, 0:T], S2[:, c, 0:T], RSTD[:, 0:T], ALU.mult, ["RSTD"], ["S2_%d" % c])
                    ACT(MIX[:, c, 0:T], S2[:, c, 0:T], AF.Silu, ["S2_%d" % c], ["MIX%d" % c],
                        scale=PRM[:, gcol + c:gcol + c + 1], bias=PRM[:, bcol + c:bcol + c + 1])
                    yield
            bg.append(ln_a_gen())

            phase_switch(ARENA_KEYS_M2, ARENA_KEYS_M1)
            units = []
            for mb in range(4):
                for b in (1, 2, 3, 0):
                    units.append((mb, b))
            pend = None
            for ui, (mb, b) in enumerate(units):
                oi = ui % 4
                if ui == 3:
                    drain()
                    phase_switch(["MG%d" % m_ for m_ in range(16)])
                wv, wk = wload([(w_in[l][:, GATES + b * 2048 + mb * 512:GATES + b * 2048 + (mb + 1) * 512], 16, 512)])
                bi = cnt["brw"] % 3
                cnt["brw"] += 1
                bw, bk = BRW[bi], "BRW%d" % bi
                DMA("pool", "C" + bk,
                    [(bw, w_br[l][b * 512:(b + 1) * 512, mb * 512:(mb + 1) * 512].rearrange("(k p) n -> p k n", p=128))],
                    [], [bk])
                for mi in range(4):
                    m = mb * 4 + mi
                    tick()
                    gg = next_ps()
                    proj_chunk(wv, wk, mi * 128, 16, xn_rhs, xn_keys, T, gg)
                    si = cnt["alt"] % 2
                    cnt["alt"] += 1
                    ACT(SIG[si][:, 0:T], PSG[gg][:, 0:T], AF.Sigmoid, [], ["PS%d" % gg, "SIG%d" % si])
                    gbr = next_ps()
                    for kc in range(4):
                        for (a, b_) in tiles(T):
                            MM(PSG[gbr][:, a:b_], bw[:, kc, mi * 128:(mi + 1) * 128], MIX[:, b * 4 + kc, a:b_],
                               kc == 0, kc == 3, [bk, "MIX%d" % (b * 4 + kc)], ["PS%d" % gbr])
                    if oi == 0:
                        TT(ACC[mi][:, 0:T], PSG[gbr][:, 0:T], SIG[si][:, 0:T], ALU.mult, ["SIG%d" % si],
                           ["PS%d" % gbr, "ACC%d" % mi])
                    else:
                        TT(PROD[:, 0:T], PSG[gbr][:, 0:T], SIG[si][:, 0:T], ALU.mult, ["SIG%d" % si],
                           ["PS%d" % gbr, "PROD"])
                        if oi < 3:
                            TT(ACC[mi][:, 0:T], ACC[mi][:, 0:T], PROD[:, 0:T], ALU.add, ["PROD"], ["ACC%d" % mi])
                        else:
                            TT(MG[:, m, 0:T], ACC[mi][:, 0:T], PROD[:, 0:T], ALU.add, ["PROD", "ACC%d" % mi],
                               ["MG%d" % m])
                    pump(6)
            def hsf_group(pc, l=l):
                rows_in_multi([sf_in[l][0:2 * NS, (4 * pc + q) * 128:(4 * pc + q + 1) * 128] for q in range(4)],
                              2 * NS, [(HSF[:, 4 * pc + q, :, :], "HSF") for q in range(4)], 2)
            if NS:
                phase_switch(["HSF"], MIXK)
                for pc in range(22):
                    later(lambda pc=pc: hsf_group(pc), 1 + (pc * 16) // 22)
            for mb in range(4):
                wv, wk = wload([(w_o[l][:, mb * 512:(mb + 1) * 512], 16, 512)])
                for mi in range(4):
                    m = mb * 4 + mi
                    tick()
                    g = next_ps()
                    proj_chunk(wv, wk, mi * 128, 16, lambda k: MG[:, k, :], lambda k: ["MG%d" % k], T, g)
                    TT(X[:, m, 0:T], PSG[g][:, 0:T], X[:, m, 0:T], ALU.add, [], ["PS%d" % g, "X%d" % m])

            phase_switch(ARENA_KEYS_FFN, ARENA_KEYS_M2 + MIXK)
            rmsnorm(T, O_NF + 16 * l, lambda c: XN[:, c, 0:T], lambda c: "XN%d" % c, "ffn")
            post_ops = []
            for jb in range(11):
                for half in range(2):
                    col0 = half * 5632 + jb * 512
                    wv, wk = wload([(f_up[l][:, col0:col0 + 512], 16, 512)])
                    for ji in range(4):
                        ch = half * 44 + jb * 4 + ji
                        hb, hk = HB[ji % 2], "HB%d" % (ji % 2)
                        CP(hb[:, 0:2], HISTF[:, l, ch, :], ["HF%d" % l], [hk], eng="act")
                        if NS:
                            hv = hb[:, 2 + TP:2 + TP + NS * 6].rearrange("p (s l) -> p s l", l=6)
                            CP(hv[:, :, 0:2], HSF[:, ch, :, :], ["HSF"], [hk], eng="act")
                        tick()
                        g = next_ps()
                        proj_chunk(wv, wk, ji * 128, 16, xn_rhs, xn_keys, T, g)

                        def emh(dst, lo, hi, smp, g=g, hk=hk):
                            ACT(dst, v3(PSG[g][:, lo:hi], smp), AF.Copy, [], ["PS%d" % g, hk])
                        split_evac(hb, hk, 2, TP, NS, emh)
                        _cv, _ck = CVB[ji % 2], "CVB%d" % (ji % 2)
                        _wc2 = O_FDW + 264 * l + 3 * ch + 2
                        ACT(_cv[:, 0:T], PSG[g][:, 0:T], AF.Copy, [], ["PS%d" % g, _ck], scale=PRM[:, _wc2:_wc2 + 1])
                        for fn in post_ops:
                            fn()
                        del post_ops[:]
                        wc = O_FDW + 264 * l + 3 * ch
                        cv, ck = CVB[ji % 2], "CVB%d" % (ji % 2)
                        for k in (0, 1):
                            wk_ = PRM[:, wc + k:wc + k + 1]
                            STT(cv[:, 0:TP], hb[:, k:k + TP], wk_, cv[:, 0:TP], ALU.mult, ALU.add, [hk], [ck])
                            if NS:
                                cvs = cv[:, TP:T].rearrange("p (s l) -> p s l", l=4)
                                STT(cvs, hv[:, :, k:k + 4], wk_, cvs, ALU.mult, ALU.add, [hk], [ck])
                        if half == 0:
                            post_ops.append(lambda cv=cv, ck=ck, ji=ji: ACT(SGB[:, ji, 0:T], cv[:, 0:T], AF.Silu, [ck],
                                                                            ["SGB%d" % ji]))
                        else:
                            TT(ACTB[:, ji, 0:T], SGB[:, ji, 0:T], cv[:, 0:T], ALU.mult, ["SGB%d" % ji, ck], ["ACTB"])
                        if not last:
                            post_ops.append(lambda hb=hb, hk=hk, ch=ch, l=l: CP(HISTF[:, l, ch, :], hb[:, TP:TP + 2], [hk],
                                                                                ["HF%d" % l], eng="act"))
                        else:
                            sg, sgk = STG[(2 * jb + half) % 2], "STG%d" % ((2 * jb + half) % 2)

                            def stg_copies(sg=sg, sgk=sgk, hb=hb, hk=hk, hv=hv, ji=ji):
                                CP(sg[:, ji, 0:2], hb[:, TP:TP + 2], [hk], [sgk], eng="act")
                                CP(sg[:, ji, 2:2 + 2 * NS].rearrange("p (s k) -> p s k", k=2), hv[:, :, 4:6], [hk],
                                   [sgk], eng="act")
                            post_ops.append(stg_copies)
                    if last:
                        chb = half * 44 + jb * 4
                        later(lambda sg=sg, sgk=sgk, chb=chb, l=l: rows_out_multi(
                            [(sg[:, q, 0:2 + 2 * NS], sgk, 2 + 2 * NS) for q in range(4)],
                            [of_out[l][:, (chb + q) * 128:(chb + q + 1) * 128] for q in range(4)]), 3)
                def down(jb=jb, l=l):
                    wv, wk = wload([(f_dn[l][jb * 512:(jb + 1) * 512, :], 4, 2048)])
                    for m in range(16):
                        g = next_ps()
                        for k in range(4):
                            for (a, b) in tiles(T):
                                MM(PSG[g][:, a:b], wv[:, k, m * 128:(m + 1) * 128], ACTB[:, k, a:b], k == 0, k == 3,
                                   [wk, "ACTB"], ["PS%d" % g])
                        TT(X[:, m, 0:T], PSG[g][:, 0:T], X[:, m, 0:T], ALU.add, [], ["PS%d" % g, "X%d" % m])
                later(down, 3)
            for fn in post_ops:
                fn()
            del post_ops[:]
            flush()

        if not last:
            ng = [(r0, nr, col, fb) for (r0, nr, col) in row_tiles_of(pi + 1) for fb in range(4)]
            xpre[pi + 1] = [xload_issue(gp, True) for gp in ng[:3]]
        for fb in range(4):
            if fb == 0:
                g = next_ps()
                for c in range(16):
                    ti = c % 2
                    fb_ = FT[ti].bitcast(BF16)
                    ACT(fb_[:, 0:T], X[:, c, 0:T], AF.Square, ["X%d" % c], ["FT%d" % ti])
                    for (a, b) in tiles(T):
                        MM(PSG[g][:, a:b], ONESA_H, fb_[:, a:b], c == 0, c == 15, ["FT%d" % ti], ["PS%d" % g])
                ACT(RSTD[:, 0:T], PSG[g][:, 0:T], AF.Sqrt, [], ["PS%d" % g, "RSTD"], bias=EPSB[:, 0:1], scale=1.0)
                RECIP(RSTD[:, 0:T], RSTD[:, 0:T], [], ["RSTD"])
            for c in range(4):
                cc = fb * 4 + c
                STT(YN[:, c, 0:T], X[:, cc, 0:T], PRM[:, O_NFIN + cc:O_NFIN + cc + 1], RSTD[:, 0:T], ALU.mult,
                    ALU.mult, ["X%d" % cc, "RSTD"], ["YN%d" % c])
            for (r0, nr, col) in row_tiles:
                g = next_ps()
                for c in range(4):
                    TRP(PSG[g][0:nr, c * 128:(c + 1) * 128], YN[:, c, col:col + nr], IDENT, ["YN%d" % c],
                        ["PS%d" % g])
                i = cnt["yob"] % 4
                cnt["yob"] += 1
                CP(YOB[i][0:nr, 0:512], PSG[g][0:nr, 0:512], [], ["PS%d" % g, "YOB%d" % i], eng=alt_eng())
                DMA("sp", "CYOB%d" % i, [(y_out[r0:r0 + nr, fb * 512:(fb + 1) * 512], YOB[i][0:nr, 0:512])],
                    ["YOB%d" % i], [])

    tr.finalize()
    sem_names = ["E" + e for e in ENGS] + sorted(tr.chan_cnt.keys())
    final_waits = [(ch, 16 * n) for ch, n in tr.chan_cnt.items()]
    import contextlib
    with contextlib.ExitStack() as es:
        sems = {n: es.enter_context(nc.semaphore(n)) for n in sem_names}
        block = es.enter_context(nc.Block())

        @block.tensor
        def _(e):
            tr.emit("pe", e, sems)

        @block.scalar
        def _(e):
            tr.emit("act", e, sems)

        @block.vector
        def _(e):
            tr.emit("dve", e, sems)

        @block.gpsimd
        def _(e):
            tr.emit("pool", e, sems)

        @block.sync
        def _(e):
            tr.emit("sp", e, sems)
            for ch, v in final_waits:
                e.wait_ge(sems[ch], v)
    n_ins = {e: len(tr.streams[e]) for e in ENGS}
    print("kernel: ops per engine", n_ins, "arena words", m1_end, m2_end, ffn_end, "of", NW, flush=True)
    return nc


_NC_CACHE = {}


def _fm(v, n):
    return np.ascontiguousarray(v.reshape(n, 128).T)


def kernel(**inp):
    f = lambda k: np.asarray(inp[k], dtype=np.float32)
    x_prompt, x_sample = f("x_prompt"), f("x_sample")
    st_c, st_s, st_p, st_f = f("state_conf_conv"), f("state_sconv"), f("state_pool"), f("state_ffn_conv")
    prm = np.zeros((128, NPRM), np.float32)
    for l in range(2):
        prm[:, O_NM + 16 * l:O_NM + 16 * l + 16] = _fm(f("norm_mix")[l], 16)
        prm[:, O_NF + 16 * l:O_NF + 16 * l + 16] = _fm(f("norm_ffn")[l], 16)
        prm[:, O_CDW + 124 * l:O_CDW + 124 * (l + 1)] = f("conf_dw")[l].reshape(31, 4, 128).transpose(2, 1, 0).reshape(128, 124)
        prm[:, O_CLG + 4 * l:O_CLG + 4 * l + 4] = _fm(f("conf_ln_g")[l], 4)
        prm[:, O_CLB + 4 * l:O_CLB + 4 * l + 4] = _fm(f("conf_ln_b")[l], 4)
        prm[:, O_GLG + 4 * l:O_GLG + 4 * l + 4] = _fm(f("gmlp_ln_g")[l], 4)
        prm[:, O_GLB + 4 * l:O_GLB + 4 * l + 4] = _fm(f("gmlp_ln_b")[l], 4)
        prm[:, O_SDW + 12 * l:O_SDW + 12 * (l + 1)] = f("sconv_dw")[l].reshape(3, 4, 128).transpose(2, 1, 0).reshape(128, 12)
        prm[:, O_PSC + 4 * l:O_PSC + 4 * l + 4] = _fm(f("pool_scale")[l], 4)
        prm[:, O_FDW + 264 * l:O_FDW + 264 * (l + 1)] = f("ffn_dw")[l].reshape(3, 88, 128).transpose(2, 1, 0).reshape(128, 264)
    prm[:, O_NFIN:O_NFIN + 16] = _fm(f("norm_final"), 16)
    cst = np.zeros((128, 320), np.float32)
    cst[:, 0:128] = np.eye(128, dtype=np.float32)
    cst[:, 128:256] = np.triu(np.ones((128, 128), np.float32))
    for g, win in enumerate((2, 4, 8, 16)):
        cst[:, 256 + 16 * g:256 + 16 * (g + 1)] = 1.0 / np.minimum(win, np.arange(16) + 1.0)
    ws, gbias = f("gmlp_ws"), f("gmlp_b")
    wst = np.ascontiguousarray(ws.transpose(3, 0, 1, 2).reshape(128, 1024))
    gbb = np.ascontiguousarray(np.broadcast_to(gbias.reshape(1, 1024), (128, 1024)))
    sw = np.ascontiguousarray(np.broadcast_to(
        np.concatenate([ws[:, :, :4, :4].reshape(1, 128), gbias[:, :, :4].reshape(1, 32)], axis=1), (128, 160)))
    pw = np.ascontiguousarray(f("pool_w").transpose(2, 0, 1, 3).reshape(128, 1024))
    shared = dict(w_in=f("w_in"), w_br=np.ascontiguousarray(f("w_branch").reshape(2, 2048, 2048)), w_o=f("w_o"),
                  f_up=f("ffn_up"), f_dn=f("ffn_down"), prm=prm, cst=cst, wst=wst, gbb=gbb, sw=sw, pw=pw)
    in_maps = []
    for core in range(NCORE):
        sq, half = core // 2, core % 2
        w0 = 0 if half == 0 else 896
        sl = slice(core * NSQ, (core + 1) * NSQ)
        xw = np.concatenate([x_prompt[sq, w0:w0 + TPW], x_sample[sl].reshape(NSQ * 4, 2048)], axis=0)
        m = dict(shared)
        m.update(xw=np.ascontiguousarray(xw),
                 sc=np.ascontiguousarray(st_c[:, sl].reshape(2, NSQ * 30, 512)),
                 ss=np.ascontiguousarray(st_s[:, sl].reshape(2, NSQ * 2, 512)),
                 sp=np.ascontiguousarray(st_p[:, sl].reshape(2, NSQ * 15, 512)),
                 sf=np.ascontiguousarray(st_f[:, sl].reshape(2, NSQ * 2, 11264)))
        in_maps.append(m)
    if "nc" not in _NC_CACHE:
        _NC_CACHE["nc"] = build_nc()
    res = run_bass_kernel_spmd(_NC_CACHE["nc"], in_maps, core_ids=list(range(NCORE)))
    R = res.results
    y_prompt = np.zeros((4, 2048, 2048), np.float32)
    y_sample = np.zeros((128, 4, 2048), np.float32)
    conf_p = np.zeros((2, 4, 30, 512), np.float32)
    conf_s = np.zeros((2, 128, 30, 512), np.float32)
    sconv_p = np.zeros((2, 4, 2, 512), np.float32)
    sconv_s = np.zeros((2, 128, 2, 512), np.float32)
    pool_p = np.zeros((2, 4, 15, 512), np.float32)
    pool_s = np.zeros((2, 128, 15, 512), np.float32)
    ffn_p = np.zeros((2, 4, 2, 11264), np.float32)
    ffn_s = np.zeros((2, 128, 2, 11264), np.float32)
    v_s = np.zeros((2, 128, 4, 512), np.float32)
    for core in range(NCORE):
        sq, half = core // 2, core % 2
        sl = slice(core * NSQ, (core + 1) * NSQ)
        r = R[core]
        if half == 0:
            y_prompt[sq, 0:1026] = r["y"][0:1026]
        else:
            y_prompt[sq, 1026:2048] = r["y"][130:1152]
            conf_p[:, sq] = r["oc"][:, 0:30]
            sconv_p[:, sq] = r["os"][:, 0:2]
            pool_p[:, sq] = r["op"][:, 0:15]
            ffn_p[:, sq] = r["of"][:, 0:2]
        y_sample[sl] = r["y"][1152:1216].reshape(NSQ, 4, 2048)
        conf_s[:, sl] = r["oc"][:, 30:].reshape(2, NSQ, 30, 512)
        sconv_s[:, sl] = r["os"][:, 2:].reshape(2, NSQ, 2, 512)
        pool_s[:, sl] = r["op"][:, 15:].reshape(2, NSQ, 15, 512)
        ffn_s[:, sl] = r["of"][:, 2:].reshape(2, NSQ, 2, 11264)
        v_s[:, sl] = r["ov"].reshape(2, NSQ, 4, 512)
    return (y_prompt, y_sample, conf_p, conf_s, sconv_p, sconv_s, pool_p, pool_s, ffn_p, ffn_s, v_s)
```
